# Optimizing a Trainium2 kernel written in Bass

```python
import math
import jax, jax.numpy as jnp
from jax import lax
import numpy as np

D_MODEL = 1024
BATCH = 32
SEQ = 2048
DEPTH = 1

CHUNK = 64
QBLOCK = 128
HEAD_DIM = 64
MIX_WIDTH = D_MODEL
A_HEADS = 4
A_WIDTH = A_HEADS * 2 * HEAD_DIM
B_HEADS = 8
B_WIDTH = B_HEADS * HEAD_DIM
IN_WIDTH = 6 * 512
D_FF = ((8 * D_MODEL // 3 + 255) // 256) * 256
PLE_DIM = 256
ROPE_THETA = 10000.0
LN_EPS = 1e-5
DEEPNORM_ALPHA = (2.0 * DEPTH) ** 0.25
DEEPNORM_BETA = (8.0 * DEPTH) ** -0.25

kernel_name = "hymba_diff_stickbreak_deepnorm_encoder"


def layer_norm(x, g, b):
    xf = x.astype(jnp.float32)
    mu = jnp.mean(xf, axis=-1, keepdims=True)
    var = jnp.mean(jnp.square(xf - mu), axis=-1, keepdims=True)
    y = (xf - mu) * lax.rsqrt(var + LN_EPS)
    return (y * g.astype(jnp.float32) + b.astype(jnp.float32)).astype(x.dtype)


def rms_norm(x, g):
    xf = x.astype(jnp.float32)
    y = xf * lax.rsqrt(jnp.mean(jnp.square(xf), axis=-1, keepdims=True) + LN_EPS)
    return y * g.astype(jnp.float32)


def rope_tables(seq_len):
    inv_freq = 1.0 / (ROPE_THETA ** (jnp.arange(0, HEAD_DIM, 2, dtype=jnp.float32) / HEAD_DIM))
    ang = jnp.arange(seq_len, dtype=jnp.float32)[:, None] * inv_freq[None, :]
    ang = jnp.concatenate([ang, ang], axis=-1)
    return jnp.cos(ang), jnp.sin(ang)


def apply_rope(t, cos, sin):
    tf = t.astype(jnp.float32)
    t1, t2 = jnp.split(tf, 2, axis=-1)
    rot = jnp.concatenate([-t2, t1], axis=-1)
    return tf * cos[None, :, None, :] + rot * sin[None, :, None, :]


def differential_attention(q, k, v, lam):
    S = q.shape[1]
    scale = 1.0 / math.sqrt(HEAD_DIM)
    vf = v.astype(jnp.float32)
    outs = []
    for blk in range(S // QBLOCK):
        t0, t1 = blk * QBLOCK, (blk + 1) * QBLOCK
        qpos = jnp.arange(t0, t1)
        kpos = jnp.arange(t1)
        mask = (qpos[:, None] // CHUNK) >= (kpos[None, :] // CHUNK)
        s = jnp.einsum('bqhmd,bkhmd->bhmqk', q[:, t0:t1], k[:, :t1]) * scale
        s = jnp.where(mask[None, None, None], s, -jnp.inf)
        pr = jax.nn.softmax(s, axis=-1)
        w = pr[:, :, 0] - lam * pr[:, :, 1]
        outs.append(jnp.einsum('bhqk,bkhe->bqhe', w, vf[:, :t1]))
    return jnp.concatenate(outs, axis=1)


def stick_breaking_attention(q, k, v):
    S = q.shape[1]
    scale = 1.0 / math.sqrt(HEAD_DIM)
    qf, kf, vf = q.astype(jnp.float32), k.astype(jnp.float32), v.astype(jnp.float32)
    outs = []
    for blk in range(S // QBLOCK):
        t0, t1 = blk * QBLOCK, (blk + 1) * QBLOCK
        qpos = jnp.arange(t0, t1)
        kpos = jnp.arange(t1)
        mask = (kpos[None, :] < qpos[:, None])[None, None]
        z = jnp.einsum('bqhd,bkhd->bhqk', qf[:, t0:t1], kf[:, :t1]) * scale
        log_beta = jax.nn.log_sigmoid(z)
        log_1m_beta = jnp.where(mask, -jax.nn.softplus(z), 0.0)
        suffix = lax.cumsum(log_1m_beta, axis=3, reverse=True) - log_1m_beta
        a = jnp.where(mask, jnp.exp(log_beta + suffix), 0.0)
        outs.append(jnp.einsum('bhqk,bkhd->bqhd', a, vf[:, :t1]))
    return jnp.concatenate(outs, axis=1)


def setup_inputs(seed: int = 0) -> dict:
    key = jax.random.key(seed)
    ks = jax.random.split(key, 24)
    f32 = jnp.float32
    nrm = lambda k, shape, s: jax.random.normal(k, shape, f32) * s
    gain = lambda k, shape: 1.0 + 0.02 * jax.random.normal(k, shape, f32)
    L, D = DEPTH, D_MODEL
    return {
        "x": jax.random.normal(ks[0], (BATCH, SEQ, D), f32),
        "p": jax.random.normal(ks[1], (DEPTH, BATCH, SEQ, PLE_DIM), f32),
        "ln_emb_g": gain(ks[2], (D,)),
        "ln_emb_b": nrm(ks[3], (D,), 0.02),
        "w_in": nrm(ks[4], (L, D, IN_WIDTH), D ** -0.5),
        "lam_q1": nrm(ks[5], (L, HEAD_DIM), 0.1),
        "lam_k1": nrm(ks[6], (L, HEAD_DIM), 0.1),
        "lam_q2": nrm(ks[7], (L, HEAD_DIM), 0.1),
        "lam_k2": nrm(ks[8], (L, HEAD_DIM), 0.1),
        "subln_g": gain(ks[9], (L, 2 * HEAD_DIM)),
        "w_out": nrm(ks[10], (L, MIX_WIDTH, D), DEEPNORM_BETA * MIX_WIDTH ** -0.5),
        "ln1_g": gain(ks[11], (L, D)),
        "ln1_b": nrm(ks[12], (L, D), 0.02),
        "w_ffn_gate": nrm(ks[13], (L, D, D_FF), D ** -0.5),
        "w_ffn_up": nrm(ks[14], (L, D, D_FF), D ** -0.5),
        "w_ffn_down": nrm(ks[15], (L, D_FF, D), DEEPNORM_BETA * D_FF ** -0.5),
        "ln2_g": gain(ks[16], (L, D)),
        "ln2_b": nrm(ks[17], (L, D), 0.02),
        "w_ple_gate": nrm(ks[18], (L, D, D), D ** -0.5),
        "b_ple_gate": nrm(ks[19], (L, D), 0.02),
        "w_ple_proj": nrm(ks[20], (L, PLE_DIM, D), DEEPNORM_BETA * PLE_DIM ** -0.5),
        "ln3_g": gain(ks[21], (L, D)),
        "ln3_b": nrm(ks[22], (L, D), 0.02),
    }


def reference(x, p, ln_emb_g, ln_emb_b, w_in, lam_q1, lam_k1, lam_q2, lam_k2, subln_g, w_out,
              ln1_g, ln1_b, w_ffn_gate, w_ffn_up, w_ffn_down, ln2_g, ln2_b,
              w_ple_gate, b_ple_gate, w_ple_proj, ln3_g, ln3_b):
    B, S, _ = x.shape
    cos, sin = rope_tables(S)
    h = layer_norm(x, ln_emb_g, ln_emb_b)
    for i in range(DEPTH):
        lambda_init = 0.8 - 0.6 * math.exp(-0.3 * i)
        proj = h @ w_in[i]
        aq, ak, av, bq, bk, bv = jnp.split(proj, 6, axis=-1)
        aq = apply_rope(aq.reshape(B, S, 2 * A_HEADS, HEAD_DIM), cos, sin).reshape(B, S, A_HEADS, 2, HEAD_DIM)
        ak = apply_rope(ak.reshape(B, S, 2 * A_HEADS, HEAD_DIM), cos, sin).reshape(B, S, A_HEADS, 2, HEAD_DIM)
        av = av.reshape(B, S, A_HEADS, 2 * HEAD_DIM)
        lam = (jnp.exp(jnp.sum(lam_q1[i].astype(jnp.float32) * lam_k1[i].astype(jnp.float32)))
               - jnp.exp(jnp.sum(lam_q2[i].astype(jnp.float32) * lam_k2[i].astype(jnp.float32)))
               + lambda_init)
        oa = differential_attention(aq, ak, av, lam)
        oa = rms_norm(oa, subln_g[i]) * (1.0 - lambda_init)
        ob = stick_breaking_attention(bq.reshape(B, S, B_HEADS, HEAD_DIM),
                                      bk.reshape(B, S, B_HEADS, HEAD_DIM),
                                      bv.reshape(B, S, B_HEADS, HEAD_DIM))
        mix = jnp.concatenate([oa.reshape(B, S, A_WIDTH), ob.reshape(B, S, B_WIDTH)], axis=-1).astype(h.dtype)
        h = layer_norm(DEEPNORM_ALPHA * h + mix @ w_out[i], ln1_g[i], ln1_b[i])
        f = (jax.nn.silu(h @ w_ffn_gate[i]) * (h @ w_ffn_up[i])) @ w_ffn_down[i]
        h = layer_norm(DEEPNORM_ALPHA * h + f, ln2_g[i], ln2_b[i])
        gate = jax.nn.sigmoid(h @ w_ple_gate[i] + b_ple_gate[i])
        e = p[i] @ w_ple_proj[i]
        h = layer_norm(DEEPNORM_ALPHA * h + gate * e, ln3_g[i], ln3_b[i])
    return h
```

```python
import contextlib
import math
import numpy as np
import concourse.bass as bass
import concourse.mybir as mybir
from concourse.bass_utils import run_bass_kernel_spmd

F32 = mybir.dt.float32
BF16 = mybir.dt.bfloat16
AF = mybir.ActivationFunctionType
ALU = mybir.AluOpType
AX = mybir.AxisListType

NCORES = 8
BATCH = 32
S = 2048
D = 1024
FF = 2816
PD = 256
NSEQ = BATCH // NCORES
NT = S // 128
TB = 512
NB = S // TB
EPS = 1e-5
ALPHA = 2.0 ** 0.25
LAMBDA_INIT = 0.8 - 0.6 * math.exp(0.0)
SCALE = 0.125
PREP_INFLIGHT = 16

VEC_NAMES = ["ln_emb_g", "ln_emb_b", "ln1_g", "ln1_b", "ln2_g", "ln2_b", "ln3_g", "ln3_b", "b_ple_gate"]


class Sem:
    def __init__(self, h, name):
        self.h = h
        self.name = name
        self.count = 0


class Buf:
    __slots__ = ("name", "w", "r", "excl")

    def __init__(self, name, excl=False):
        self.name = name
        self.w = None
        self.r = {}
        self.excl = excl


class Eng:
    def __init__(self, name, h, sem, is_pe=False):
        self.name = name
        self.h = h
        self.sem = sem
        self.is_pe = is_pe
        self.seen = {}


class Sched:
    def __init__(self, nc, es):
        self.nc = nc
        self.es = es
        self.nsem = 0
        self.dma_sems = []
        mk = self.new_sem
        self.PE = Eng("pe", nc.tensor, mk("e_pe"), True)
        self.ACT = Eng("act", nc.scalar, mk("e_act"))
        self.DVE = Eng("dve", nc.vector, mk("e_dve"))
        self.POOL = Eng("pool", nc.gpsimd, mk("e_pool"))
        self.SP = Eng("sp", nc.sync, mk("e_sp"))
        self.engines = [self.PE, self.ACT, self.DVE, self.POOL, self.SP]

    def new_sem(self, name):
        h = self.es.enter_context(self.nc.semaphore(name))
        self.nsem += 1
        return Sem(h, name)

    def new_dma_sem(self, name):
        s = self.new_sem(name)
        self.dma_sems.append(s)
        return s

    def _waits(self, eng, reads, writes):
        need = {}

        def add(t):
            if t is None:
                return
            sem, val = t
            if eng.is_pe and sem is eng.sem:
                return
            cur = need.get(sem.name)
            if cur is None or cur[1] < val:
                need[sem.name] = (sem, val)

        for b in reads:
            add(b.w)
            if b.excl:
                for t in b.r.values():
                    add(t)
        for b in writes:
            add(b.w)
            for t in b.r.values():
                add(t)
        for name, (sem, val) in need.items():
            if eng.seen.get(name, 0) < val:
                eng.h.wait_ge(sem.h, val)
                eng.seen[name] = val

    def _record(self, t, reads, writes):
        sem, val = t
        for b in reads:
            if b.excl:
                b.w = t
                b.r = {}
                continue
            cur = b.r.get(sem.name)
            if cur is None or cur[1] < val:
                b.r[sem.name] = t
        for b in writes:
            b.w = t
            b.r = {}

    def op(self, eng, fn, reads=(), writes=()):
        self._waits(eng, reads, writes)
        ins = fn()
        eng.sem.count += 1
        ins.then_inc(eng.sem.h, 1)
        t = (eng.sem, eng.sem.count)
        self._record(t, reads, writes)
        return t

    def pe_group(self, fns, reads=(), writes=()):
        eng = self.PE
        self._waits(eng, reads, writes)
        ins = None
        for fn in fns:
            ins = fn()
        eng.sem.count += 1
        ins.then_inc(eng.sem.h, 1)
        t = (eng.sem, eng.sem.count)
        self._record(t, reads, writes)
        return t

    def dma(self, q, sem, out, in_, reads=(), writes=()):
        self._waits(q, reads, writes)
        ins = q.h.dma_start(out=out, in_=in_)
        sem.count += 16
        ins.then_inc(sem.h, 16)
        t = (sem, sem.count)
        self._record(t, reads, writes)
        return t

    def barrier(self, exclude=()):
        tickets = [(e.sem, e.sem.count) for e in self.engines if e.sem.count > 0]
        tickets += [(s, s.count) for s in self.dma_sems if s.count > 0 and s not in exclude]
        for e in self.engines:
            for sem, val in tickets:
                if e.is_pe and sem is e.sem:
                    continue
                if e.seen.get(sem.name, 0) < val:
                    e.h.wait_ge(sem.h, val)
                    e.seen[sem.name] = val


class _Stop(Exception):
    pass


class Arena:
    def __enter__(self):
        self.st = contextlib.ExitStack()
        return self.st

    def __exit__(self, et, ev, tb):
        self.st.close()
        return False


def build(nseq=NSEQ, dbg=False, stage=None):
    nc = bass.Bass("TRN2", target_bir_lowering=False)
    dt = nc.dram_tensor
    x_d = dt("x", [nseq, S, D], F32, kind="ExternalInput").ap()
    p_d = dt("p", [nseq, S, PD], F32, kind="ExternalInput").ap()
    w_in_d = dt("w_in", [D, 3072], F32, kind="ExternalInput").ap()
    w_out_d = dt("w_out", [D, D], F32, kind="ExternalInput").ap()
    w_g_d = dt("w_ffn_gate", [D, FF], F32, kind="ExternalInput").ap()
    w_u_d = dt("w_ffn_up", [D, FF], F32, kind="ExternalInput").ap()
    w_d_d = dt("w_ffn_down", [FF, D], F32, kind="ExternalInput").ap()
    w_pg_d = dt("w_ple_gate", [D, D], F32, kind="ExternalInput").ap()
    w_pp_d = dt("w_ple_proj", [PD, D], F32, kind="ExternalInput").ap()
    vecs_d = dt("vecs", [9, 128, D], F32, kind="ExternalInput").ap()
    small_d = dt("small", [128, 384], F32, kind="ExternalInput").ap()
    cbf_d = dt("cbf", [128, 640], F32, kind="ExternalInput").ap()
    cs_d = dt("cossin", [2, 128, S], F32, kind="ExternalInput").ap()
    out_d = dt("out", [nseq, S, D], F32, kind="ExternalOutput").ap()
    win_s = dt("win_s", [8, D, 384], BF16).ap()
    wout_s = dt("wout_s", [D, D], BF16).ap()
    wg_s = dt("wg_s", [D, FF], BF16).ap()
    wu_s = dt("wu_s", [D, FF], BF16).ap()
    wd_s = dt("wd_s", [FF, D], BF16).ap()
    wpg_s = dt("wpg_s", [D, D], BF16).ap()
    wpp_s = dt("wpp_s", [PD, D], BF16).ap()

    es = contextlib.ExitStack()
    sc = Sched(nc, es)
    PE, ACT, DVE, POOL, SP = sc.PE, sc.ACT, sc.DVE, sc.POOL, sc.SP

    uniq = [0]

    def sb(stack, name, shape, dtype):
        uniq[0] += 1
        return stack.enter_context(nc.sbuf_tensor(f"{name}_u{uniq[0]}", shape, dtype))

    cbf = sb(es, "cbf_t", [128, 640], BF16)
    g08 = sb(es, "g08", [128, 128], F32)
    lamw = sb(es, "lamw", [128, 8], F32)
    hT = sb(es, "hT", [128, 8, S], BF16)
    mixT = sb(es, "mixT", [128, NB, 8, TB], BF16)
    wsl = [sb(es, f"wsl{i}", [128, 4096], BF16) for i in range(2)]
    psb = [es.enter_context(nc.psum_tensor(f"psb{i}", [128, 512], F32)) for i in range(7)]
    psT = es.enter_context(nc.psum_tensor("psT", [128, 1024], BF16))
    b_ps = [Buf(f"psb{i}", excl=True) for i in range(7)]
    b_psT = Buf("psT", excl=True)

    ident = cbf[:, 0:128]
    tri = cbf[:, 128:256]
    tric = cbf[:, 256:384]
    perms = cbf[:, 384:512]
    maskb = cbf[:, 512:640]
    b_const = Buf("const")
    b_gbs = [Buf(f"gb{i}") for i in range(9)]
    s_gbs = [sc.new_dma_sem(f"s_gb{i}") for i in range(9)]
    b_cs = Buf("cossin")
    s_cs = sc.new_dma_sem("s_cs")
    b_hT = [Buf(f"hT{t}") for t in range(NT)]
    b_mixT = [[Buf(f"mixT{c}_{q}") for q in range(NB)] for c in range(8)]
    b_wsl = [Buf(f"wsl{i}") for i in range(3)]
    s_wsl = [sc.new_dma_sem(f"s_wsl{i}") for i in range(3)]
    wsl_all = list(wsl)
    b_wside = Buf("wside")
    s_wside = sc.new_dma_sem("s_wside")
    b_wpp = Buf("wppt")
    s_wpp = sc.new_dma_sem("s_wpp")

    s_c = sc.new_dma_sem("s_const")
    e0 = Arena()
    e0s = e0.__enter__()
    cst32 = sb(e0s, "cst32", [128, 640], F32)
    small = sb(e0s, "small_t", [128, 384], F32)
    sc.dma(SP, s_c, cst32[:], cbf_d[:, :], writes=[b_const])
    sc.dma(SP, s_c, small[:], small_d[:, :], writes=[b_const])
    b_c2 = Buf("const2")
    sc.op(DVE, lambda: nc.vector.tensor_copy(out=cbf[:], in_=cst32[:]), reads=[b_const], writes=[b_c2])
    sc.op(DVE, lambda: nc.vector.tensor_scalar(out=g08[:], in0=small[:, 256:384], scalar1=1.0 - LAMBDA_INIT,
                                               scalar2=None, op0=ALU.mult), reads=[b_const], writes=[b_c2])
    b_l = Buf("lam")
    sc.op(DVE, lambda: nc.vector.tensor_tensor(out=cst32[:, 0:64], in0=small[:, 0:64], in1=small[:, 64:128],
                                               op=ALU.mult), reads=[b_c2, b_const], writes=[b_l])
    sc.op(DVE, lambda: nc.vector.tensor_tensor(out=cst32[:, 64:128], in0=small[:, 128:192], in1=small[:, 192:256],
                                               op=ALU.mult), reads=[b_l], writes=[b_l])
    sc.op(DVE, lambda: nc.vector.reduce_sum(out=lamw[:, 0:1], in_=cst32[:, 0:64], axis=AX.X), reads=[b_l], writes=[b_l])
    sc.op(DVE, lambda: nc.vector.reduce_sum(out=lamw[:, 1:2], in_=cst32[:, 64:128], axis=AX.X), reads=[b_l], writes=[b_l])
    sc.op(ACT, lambda: nc.scalar.activation(out=lamw[:, 2:4], in_=lamw[:, 0:2], func=AF.Exp), reads=[b_l], writes=[b_l])
    sc.op(DVE, lambda: nc.vector.tensor_tensor(out=lamw[:, 4:5], in0=lamw[:, 3:4], in1=lamw[:, 2:3], op=ALU.subtract),
          reads=[b_l], writes=[b_l])
    sc.op(DVE, lambda: nc.vector.tensor_scalar(out=lamw[:, 4:5], in0=lamw[:, 4:5], scalar1=-LAMBDA_INIT, scalar2=None,
                                               op0=ALU.add), reads=[b_l], writes=[b_l])
    neglam = lamw[:, 4:5]
    sc.barrier(exclude=())
    e0.__exit__(None, None, None)

    s_preps = [sc.new_dma_sem(f"s_prep{i}") for i in range(PREP_INFLIGHT)]
    b_prep = [Buf(f"wprep{i}") for i in range(PREP_INFLIGHT)]
    nprep = [0]

    def prep(dst, src):
        if stage == "noprep":
            return
        j = nprep[0] % PREP_INFLIGHT
        nprep[0] += 1
        sp_ = s_preps[j]
        if sp_.count > 0 and POOL.seen.get(sp_.name, 0) < sp_.count:
            POOL.h.wait_ge(sp_.h, sp_.count)
            POOL.seen[sp_.name] = sp_.count
        sc.dma(POOL, sp_, dst, src, writes=[b_prep[j]])

    qkv_off = []
    for h in range(4):
        qkv_off.append((128 * h, 512 + 128 * h, 1024 + 128 * h))
    for c in range(4):
        qkv_off.append((1536 + 128 * c, 2048 + 128 * c, 2560 + 128 * c))
    for g in range(8):
        for j in range(3):
            o = qkv_off[g][j]
            prep(win_s[g, :, j * 128:(j + 1) * 128], w_in_d[:, o:o + 128])
    for r in range(0, D, 256):
        prep(wout_s[r:r + 256, :], w_out_d[r:r + 256, :])
    for r in range(0, D, 128):
        prep(wg_s[r:r + 128, :], w_g_d[r:r + 128, :])
        prep(wu_s[r:r + 128, :], w_u_d[r:r + 128, :])
    for r in range(0, FF, 256):
        prep(wd_s[r:r + 256, :], w_d_d[r:r + 256, :])
    for r in range(0, D, 256):
        prep(wpg_s[r:r + 256, :], w_pg_d[r:r + 256, :])
    prep(wpp_s[:, :], w_pp_d[:, :])

    wcount = [0]

    def wload(nslots, view_shape, src):
        i = wcount[0] % nslots
        wcount[0] += 1
        flat = wsl_all[i]
        kc, n = view_shape
        view = flat[:, 0:kc * n].rearrange("p (k n) -> p k n", k=kc)
        sc.dma(SP, s_wsl[i], view, src, reads=b_prep, writes=[b_wsl[i]])
        return view, b_wsl[i]

    def kcview(dram2d):
        return dram2d.rearrange("(k p) n -> p k n", p=128)

    class Rot:
        def __init__(self, items):
            self.items = items
            self.i = 0

        def next(self):
            it = self.items[self.i % len(self.items)]
            self.i += 1
            return it

    def layer_norm(src, b_src, gw, bw_, stat, b_stat, outs, gmul=None, part=None):
        gmul = gmul or POOL
        st = stat[:, 0:12].rearrange("p (a b) -> p a b", a=2)
        mv = stat[:, 12:14]
        lnv = stat[:, 14:15]
        rstd = stat[:, 15:16]
        nmr = stat[:, 16:17]
        if part == "b":
            return _ln_b(src, b_src, gw, bw_, outs, gmul)
        sc.op(DVE, lambda: nc.vector.bn_stats(out=st[:, 0, :], in_=src[:, 0:512]), reads=[b_src], writes=[b_stat])
        sc.op(DVE, lambda: nc.vector.bn_stats(out=st[:, 1, :], in_=src[:, 512:1024]), reads=[b_src, b_stat], writes=[b_stat])
        sc.op(DVE, lambda: nc.vector.bn_aggr(out=mv, in_=stat[:, 0:12]), reads=[b_stat], writes=[b_stat])
        sc.op(ACT, lambda: nc.scalar.activation(out=lnv, in_=stat[:, 13:14], func=AF.Ln, bias=EPS), reads=[b_stat], writes=[b_stat])
        sc.op(ACT, lambda: nc.scalar.activation(out=rstd, in_=lnv, func=AF.Exp, scale=-0.5), reads=[b_stat], writes=[b_stat])
        sc.op(DVE, lambda: nc.vector.scalar_tensor_tensor(out=nmr, in0=stat[:, 12:13], scalar=-1.0, in1=rstd,
                                                          op0=ALU.mult, op1=ALU.mult), reads=[b_stat], writes=[b_stat])
        sc.op(ACT, lambda: nc.scalar.activation(out=src, in_=src, func=AF.Identity, bias=nmr, scale=rstd),
              reads=[b_stat], writes=[b_src])
        if part == "a":
            return
        _ln_b(src, b_src, gw, bw_, outs, gmul)

    def _ln_b(src, b_src, gw, bw_, outs, gmul):
        g_ap, g_buf = gw
        b_ap, b_buf = bw_
        sc.op(gmul, lambda: gmul.h.tensor_tensor(out=src, in0=src, in1=g_ap, op=ALU.mult),
              reads=[g_buf], writes=[b_src])
        o0, b0 = outs[0]
        if b0 is b_src:
            sc.op(DVE, lambda: nc.vector.tensor_tensor(out=o0, in0=src, in1=b_ap, op=ALU.add),
                  reads=[b_buf], writes=[b0])
        else:
            sc.op(DVE, lambda: nc.vector.tensor_tensor(out=o0, in0=src, in1=b_ap, op=ALU.add),
                  reads=[b_src, b_buf], writes=[b0])
        for o, b in outs[1:]:
            sc.op(ACT, lambda: nc.scalar.copy(out=o, in_=o0), reads=[b0], writes=[b])

    def transpose_to_hT(hbf, b_hbf, t, dst, b_dst):
        sc.pe_group([lambda c=c: nc.tensor.transpose(psT[:, c * 128:(c + 1) * 128], hbf[:, c * 128:(c + 1) * 128], ident)
                     for c in range(8)], reads=[b_hbf, b_c2], writes=[b_psT])
        sc.op(ACT, lambda: nc.scalar.copy(out=dst[:, :, t * 128:(t + 1) * 128],
                                          in_=psT[:, :].rearrange("p (c n) -> p c n", c=8)),
              reads=[], writes=[b_psT, b_dst])

    s_outs = [sc.new_dma_sem(f"s_out{i}") for i in range(12)]
    s_x = [sc.new_dma_sem("s_x0"), sc.new_dma_sem("s_x1")]
    s_p = [sc.new_dma_sem(f"s_p{i}") for i in range(4)]
    s_hb = [sc.new_dma_sem(f"s_hb{i}") for i in range(12)]
    dbg_outs = {}

    def chk(name):
        if stage == name:
            raise _Stop()

    try:
      chk("setup")
      chk("noprep")
      for b in range(nseq):
          with Arena() as ea:
              xt = [sb(ea, f"xt{i}", [128, D], F32) for i in range(2)]
              cosT = sb(ea, "cosT", [128, S], F32)
              sinT = sb(ea, "sinT", [128, S], F32)
              gba = [sb(ea, f"gba{i}", [128, D], F32) for i in range(2)]
              sc.dma(SP, s_cs, cosT[:], cs_d[0, :, :], writes=[b_cs])
              sc.dma(SP, s_cs, sinT[:], cs_d[1, :, :], writes=[b_cs])
              for k in range(2):
                  sc.dma(SP, s_gbs[k], gba[k][:], vecs_d[k, :, :], writes=[b_gbs[k]])
              hbf = [sb(ea, f"a_hbf{i}", [128, D], BF16) for i in range(2)]
              stat = [sb(ea, f"a_stat{i}", [128, 20], F32) for i in range(2)]
              qT = [sb(ea, f"qT{i}", [128, S], BF16) for i in range(2)]
              kT = [sb(ea, f"kT{i}", [128, S], BF16) for i in range(2)]
              vaug = [sb(ea, f"vaug{i}", [128, NT, 129], BF16) for i in range(2)]
              bft = [sb(ea, f"bft{i}", [128, 512], BF16) for i in range(12)]
              e32 = [sb(ea, f"e32_{i}", [128, 512], F32) for i in range(5)]
              E32 = [sb(ea, f"E32_{i}", [128, 512], F32) for i in range(6)]
              qb16 = [sb(ea, f"qb16_{i}", [128, 512], BF16) for i in range(2)]
              rt1 = [sb(ea, f"rt1_{i}", [128, 512], F32) for i in range(2)]
              rt2 = sb(ea, "rt2", [128, 512], F32)
              fin = [sb(ea, f"fin{i}", [128, 512], F32) for i in range(2)]
              mixt = [sb(ea, f"mixt{i}", [128, 4, 128], BF16) for i in range(2)]

              b_xt = [Buf("xt0"), Buf("xt1")]
              b_hbf = [Buf("hbf0"), Buf("hbf1")]
              b_stat = [Buf("stat0"), Buf("stat1")]
              b_qT = [[Buf(f"qT{i}_{q}") for q in range(NB)] for i in range(2)]
              b_kT = [[Buf(f"kT{i}_{q}") for q in range(NB)] for i in range(2)]
              b_v = [[Buf(f"v{i}_{q}") for q in range(NB)] for i in range(2)]
              b_bft = [Buf(f"bft{i}") for i in range(12)]
              b_e32 = [Buf(f"e32_{i}") for i in range(5)]
              b_E32 = [Buf(f"E32_{i}") for i in range(6)]
              b_qb16 = [Buf("qb16_0"), Buf("qb16_1")]
              b_rt1 = [Buf("rt1_0"), Buf("rt1_1")]
              b_rt2 = Buf("rt2")
              b_fo = [Buf(f"fo{i}") for i in range(4)]
              b_fsm = [Buf(f"fsm{i}") for i in range(4)]
              b_fjunk = Buf("fjunk")
              b_mixt = [Buf("mixt0"), Buf("mixt1")]
              rot_bft = Rot(list(range(12)))
              rot_e32 = Rot(list(range(5)))
              rot_E32 = Rot(list(range(6)))
              rot_T = Rot([0, 1])

              b_ones = Buf("ones")
              for i in range(2):
                  sc.op(POOL, lambda i=i: nc.gpsimd.memset(vaug[i][:, :, 128:129], 1.0), writes=[b_ones])

              GA = ((gba[0][:], b_gbs[0]), (gba[1][:], b_gbs[1]))

              def p1_a(t):
                  i = t % 2
                  sc.dma(SP, s_x[i], xt[i][:], x_d[b, t * 128:(t + 1) * 128, :], writes=[b_xt[i]])
                  layer_norm(xt[i][:], b_xt[i], GA[0], GA[1], stat[i], b_stat[i], [(hbf[i][:], b_hbf[i])], gmul=DVE, part="a")

              def p1_b(t):
                  i = t % 2
                  layer_norm(xt[i][:], b_xt[i], GA[0], GA[1], stat[i], b_stat[i], [(hbf[i][:], b_hbf[i])], gmul=DVE, part="b")

              def p1_c(t):
                  i = t % 2
                  transpose_to_hT(hbf[i], b_hbf[i], t, hT, b_hT[t])

              def phase1():
                  p0 = project_steps(0, 0)
                  p0.pop(0)()
                  p1_a(0)
                  for t in range(NT):
                      if t + 1 < NT:
                          p1_a(t + 1)
                      p1_b(t)
                      if t > 0:
                          p1_c(t - 1)
                          if (t - 1) % 4 == 3:
                              for _ in range(5):
                                  p0.pop(0)()
                  p1_c(NT - 1)
                  while p0:
                      p0.pop(0)()

              PJ = 6

              def project_steps(g, si):
                  isA = g < 4
                  steps = []
                  box = {}

                  def load():
                      box["w"] = wload(2, (8, 384), kcview(win_s[g]))
                  steps.append(load)
                  for tb in range(NB):
                      hts = b_hT[4 * tb:4 * tb + 4]
                      cols = slice(tb * TB, (tb + 1) * TB)
                      for which in range(2):
                          dstT = (qT, kT)[which][si]
                          bdst = (b_qT, b_kT)[which][si][tb]

                          def qk_step(which=which, dstT=dstT, bdst=bdst, cols=cols, hts=hts):
                              wv, bw = box["w"]
                              ps = psb[PJ]
                              sc.pe_group([lambda kc=kc: nc.tensor.matmul(
                                  ps[:, :], wv[:, kc, which * 128:(which + 1) * 128], hT[:, kc, cols],
                                  start=(kc == 0), stop=(kc == 7)) for kc in range(8)],
                                  reads=[bw] + hts, writes=[b_ps[PJ]])
                              if isA:
                                  r = which
                                  sc.op(ACT, lambda: nc.scalar.copy(out=qb16[r][:], in_=ps[:, :]), reads=[b_ps[PJ]], writes=[b_qb16[r]])
                                  sc.op(DVE, lambda: nc.vector.tensor_tensor(out=rt1[r][:], in0=ps[:, :], in1=cosT[:, cols], op=ALU.mult),
                                        reads=[b_ps[PJ], b_cs], writes=[b_rt1[r]])
                              else:
                                  sc.op(ACT, lambda: nc.scalar.copy(out=dstT[:, cols], in_=ps[:, :]), reads=[b_ps[PJ]], writes=[bdst])
                          steps.append(qk_step)

                          def rope_step(which=which, dstT=dstT, bdst=bdst, cols=cols):
                              ps = psb[PJ]
                              r = which
                              sc.pe_group([lambda: nc.tensor.matmul(ps[:, :], perms, qb16[r][:], start=True, stop=True)],
                                          reads=[b_qb16[r], b_c2], writes=[b_ps[PJ]])
                              sc.op(DVE, lambda: nc.vector.tensor_tensor(out=rt2[:], in0=ps[:, :], in1=sinT[:, cols], op=ALU.mult),
                                    reads=[b_ps[PJ], b_cs], writes=[b_rt2])
                              sc.op(DVE, lambda: nc.vector.tensor_tensor(out=dstT[:, cols], in0=rt1[r][:], in1=rt2[:], op=ALU.add),
                                    reads=[b_rt1[r], b_rt2], writes=[bdst])
                          if isA:
                              steps.append(rope_step)

                      def v_step(tb=tb, hts=hts):
                          wv, bw = box["w"]
                          ps = psb[PJ]
                          fns = []
                          for i4 in range(4):
                              t = 4 * tb + i4
                              for kc in range(8):
                                  fns.append(lambda kc=kc, i4=i4, t=t: nc.tensor.matmul(
                                      ps[:, i4 * 128:(i4 + 1) * 128], hT[:, kc, t * 128:(t + 1) * 128], wv[:, kc, 256:384],
                                      start=(kc == 0), stop=(kc == 7)))
                          sc.pe_group(fns, reads=[bw] + hts, writes=[b_ps[PJ]])
                          sc.op(ACT, lambda: nc.scalar.copy(out=vaug[si][:, 4 * tb:4 * tb + 4, 0:128],
                                                            in_=ps[:, :].rearrange("p (a n) -> p a n", a=4)),
                                reads=[b_ps[PJ]], writes=[b_v[si][tb]])
                      steps.append(v_step)
                  return steps

              class Side:
                  def __init__(self):
                      self.steps = []
                      self.tick = 0

                  def step(self, every):
                      self.tick += 1
                      if self.steps and self.tick % every == 0:
                          self.steps.pop(0)()

                  def flush(self):
                      while self.steps:
                          self.steps.pop(0)()
              side = Side()

              def mix_transpose(mi, chunk, qb):
                  sc.pe_group([lambda j=j: nc.tensor.transpose(psT[:, j * 128:(j + 1) * 128], mixt[mi][:, j, :], ident) for j in range(4)],
                              reads=[b_mixt[mi], b_c2], writes=[b_psT])
                  sc.op(ACT, lambda: nc.scalar.copy(out=mixT[:, qb, chunk, :], in_=psT[:, 0:512]),
                        reads=[], writes=[b_psT, b_mixT[chunk][qb]])

              fin_i = [0]

              def attn_A(h, si):
                  accbanks = [3, 4, 5]
                  rot_sc = Rot([0, 1, 2])

                  def acc(m, qs):
                      idx = m * 4 + qs
                      return psb[accbanks[idx // 3]][:, (idx % 3) * 129:(idx % 3) * 129 + 129]

                  b_acc = [b_ps[k] for k in accbanks]
                  for qb in range(NB):
                      tasks = [(m, kt) for m in range(2) for kt in range(4 * qb + 4)]
                      st = {}
                      started = set()

                      def first_in_bank(m, qs):
                          bank = accbanks[(m * 4 + qs) // 3]
                          if bank in started:
                              return False
                          started.add(bank)
                          return True

                      def qk(i):
                          m, kt = tasks[i]
                          r = kt - 4 * qb
                          c0 = max(0, r) * 128
                          bk = rot_sc.next()
                          pi = rot_bft.next()
                          st[i] = (bk, pi, c0, r)
                          pr = slice(m * 64, (m + 1) * 64)
                          sc.pe_group([lambda: nc.tensor.matmul(psb[bk][:, c0:512], kT[si][pr, kt * 128:(kt + 1) * 128],
                                                                qT[si][pr, qb * TB + c0:(qb + 1) * TB], start=True, stop=True)],
                                      reads=[b_kT[si][kt // 4], b_qT[si][qb]], writes=[b_ps[bk]])

                      def rest(i):
                          m, kt = tasks[i]
                          bk, pi, c0, r = st.pop(i)
                          P = bft[pi]
                          sc.op(ACT, lambda: nc.scalar.activation(out=P[:, c0:512], in_=psb[bk][:, c0:512], func=AF.Exp, scale=SCALE),
                                reads=[b_ps[bk]], writes=[b_bft[pi]])
                          if r >= 0:
                              sc.op(POOL, lambda: nc.gpsimd.memset(P[64:128, c0:c0 + 64], 0.0), writes=[b_bft[pi]])
                          fns = []
                          for qs in range(max(0, r), 4):
                              fl = first_in_bank(m, qs)
                              fns.append(lambda qs=qs, fl=fl: nc.tensor.matmul(acc(m, qs), P[:, qs * 128:(qs + 1) * 128], vaug[si][:, kt, 0:129],
                                                                               start=fl, stop=(kt == 4 * qb + qs), skip_group_check=True))
                          sc.pe_group(fns, reads=[b_bft[pi], b_v[si][kt // 4], b_ones], writes=b_acc)

                      n = len(tasks)
                      qk(0)
                      if n > 1:
                          qk(1)
                      for i in range(n):
                          rest(i)
                          if i + 2 < n:
                              qk(i + 2)
                          side.step(3)
                      fi = fin_i[0] % 2
                      fin_i[0] += 1
                      for qs in range(4):
                          a0 = acc(0, qs)
                          a1 = acc(1, qs)
                          o = fin[0][:, qs * 128:(qs + 1) * 128]
                          sm = fin[1][:, 128 + 8 * qs:136 + 8 * qs]
                          sc.op(DVE, lambda: nc.vector.reciprocal(out=sm[:, 0:1], in_=a0[:, 128:129]), reads=b_acc, writes=[b_fsm[qs]])
                          sc.op(DVE, lambda: nc.vector.reciprocal(out=sm[:, 1:2], in_=a1[:, 128:129]), reads=b_acc, writes=[b_fsm[qs]])
                          sc.op(DVE, lambda: nc.vector.tensor_tensor(out=sm[:, 2:3], in0=sm[:, 1:2], in1=neglam, op=ALU.mult),
                                reads=[b_l], writes=[b_fsm[qs]])
                          sc.op(DVE, lambda: nc.vector.tensor_scalar(out=o, in0=a1[:, 0:128], scalar1=sm[:, 2:3], scalar2=None,
                                                                     op0=ALU.mult), reads=b_acc + [b_fsm[qs]], writes=[b_fo[qs]])
                          sc.op(DVE, lambda: nc.vector.scalar_tensor_tensor(out=o, in0=a0[:, 0:128], scalar=sm[:, 0:1],
                                                                            in1=o, op0=ALU.mult, op1=ALU.add),
                                reads=b_acc + [b_fsm[qs]], writes=[b_fo[qs]])
                      for qs in range(4):
                          o = fin[0][:, qs * 128:(qs + 1) * 128]
                          sm = fin[1][:, 128 + 8 * qs:136 + 8 * qs]
                          sc.op(ACT, lambda: nc.scalar.activation(out=fin[1][:, 0:128], in_=o, func=AF.Square,
                                                                  accum_out=sm[:, 3:4]), reads=[b_fo[qs]], writes=[b_fjunk, b_fsm[qs]])
                          sc.op(ACT, lambda: nc.scalar.activation(out=sm[:, 4:5], in_=sm[:, 3:4], func=AF.Ln, bias=EPS, scale=1.0 / 128.0),
                                reads=[], writes=[b_fsm[qs]])
                          sc.op(ACT, lambda: nc.scalar.activation(out=sm[:, 5:6], in_=sm[:, 4:5], func=AF.Exp, scale=-0.5),
                                reads=[], writes=[b_fsm[qs]])
                      for qs in range(4):
                          o = fin[0][:, qs * 128:(qs + 1) * 128]
                          sm = fin[1][:, 128 + 8 * qs:136 + 8 * qs]
                          sc.op(DVE, lambda: nc.vector.scalar_tensor_tensor(out=mixt[fi][:, qs, :], in0=o, scalar=sm[:, 5:6],
                                                                            in1=g08[:], op0=ALU.mult, op1=ALU.mult),
                                reads=[b_fo[qs], b_fsm[qs], b_c2], writes=[b_mixt[fi]])
                      mix_transpose(fi, h, qb)

              def attn_B(c, si):
                  zbank = [0, 1]
                  sbank = [2, 3]
                  abank = [4, 5]
                  LA = 2
                  for qb in range(NB):
                      fi = fin_i[0] % 2
                      fin_i[0] += 1
                      kts = list(range(4 * qb + 3, -1, -1))
                      n = len(kts)
                      st = {}
                      st3 = {}

                      def s1(hh, i):
                          kt = kts[i]
                          pr = slice(hh * 64, (hh + 1) * 64)
                          r = kt - 4 * qb
                          c0 = max(0, r) * 128
                          bk = zbank[hh]
                          ei = rot_e32.next()
                          spi = rot_bft.next()
                          ti = rot_E32.next()
                          st[(hh, i)] = (ei, spi, ti, c0, r)
                          sc.pe_group([lambda: nc.tensor.matmul(psb[bk][:, c0:512], kT[si][pr, kt * 128:(kt + 1) * 128],
                                                                qT[si][pr, qb * TB + c0:(qb + 1) * TB], start=True, stop=True)],
                                      reads=[b_kT[si][kt // 4], b_qT[si][qb]], writes=[b_ps[bk]])
                          sc.op(ACT, lambda: nc.scalar.activation(out=e32[ei][:, c0:512], in_=psb[bk][:, c0:512], func=AF.Exp, scale=SCALE),
                                reads=[b_ps[bk]], writes=[b_e32[ei]])
                          sc.op(ACT, lambda: nc.scalar.activation(out=e32[ei][:, c0:512], in_=e32[ei][:, c0:512], func=AF.Ln, bias=1.0),
                                reads=[], writes=[b_e32[ei]])
                          spb = bft[spi]
                          sc.op(DVE, lambda: nc.vector.tensor_copy(out=spb[:, c0:512], in_=e32[ei][:, c0:512]),
                                reads=[b_e32[ei]], writes=[b_bft[spi]])
                          if r >= 0:
                              sc.op(POOL, lambda: nc.gpsimd.tensor_tensor(out=spb[:, c0:c0 + 128], in0=spb[:, c0:c0 + 128],
                                                                          in1=maskb, op=ALU.mult),
                                    reads=[b_c2], writes=[b_bft[spi]])
                              if c0 > 0:
                                  sc.op(POOL, lambda: nc.gpsimd.memset(spb[:, 0:c0], 0.0), writes=[b_bft[spi]])
                          sc.op(DVE, lambda: nc.vector.scalar_tensor_tensor(out=E32[ti][:, c0:512], in0=psb[bk][:, c0:512], scalar=SCALE,
                                                                            in1=e32[ei][:, c0:512], op0=ALU.mult, op1=ALU.subtract),
                                reads=[b_ps[bk], b_e32[ei]], writes=[b_E32[ti]])

                      def s2a(hh, i):
                          ei, spi, ti, c0, r = st[(hh, i)]
                          sb_ = sbank[hh]
                          sc.pe_group([lambda: nc.tensor.matmul(psb[sb_][:, :], tri, bft[spi][:, :], start=(i == 0), stop=True,
                                                                skip_group_check=True)],
                                      reads=[b_bft[spi], b_c2], writes=[b_ps[sb_]])
                          sc.op(DVE, lambda: nc.vector.tensor_tensor(out=E32[ti][:, c0:512], in0=E32[ti][:, c0:512], in1=psb[sb_][:, c0:512],
                                                                     op=ALU.subtract),
                                reads=[b_ps[sb_]], writes=[b_E32[ti]])

                      def s2b(hh, i):
                          kt = kts[i]
                          ei, spi, ti, c0, r = st.pop((hh, i))
                          sb_ = sbank[hh]
                          ab = abank[hh]
                          sc.pe_group([lambda: nc.tensor.matmul(psb[sb_][:, :], tric, bft[spi][:, :], start=False, stop=True,
                                                                skip_group_check=True)],
                                      reads=[b_bft[spi], b_c2], writes=[b_ps[sb_]])
                          ai = rot_bft.next()
                          sc.op(ACT, lambda: nc.scalar.activation(out=bft[ai][:, c0:512], in_=E32[ti][:, c0:512], func=AF.Exp),
                                reads=[b_E32[ti]], writes=[b_bft[ai]])
                          if r >= 0:
                              sc.op(POOL, lambda: nc.gpsimd.tensor_tensor(out=bft[ai][:, c0:c0 + 128], in0=bft[ai][:, c0:c0 + 128],
                                                                          in1=maskb, op=ALU.mult),
                                    reads=[b_c2], writes=[b_bft[ai]])
                          st3[(hh, i)] = (ai, r)

                      def s3(hh, i):
                          kt = kts[i]
                          ai, r = st3.pop((hh, i))
                          ab = abank[hh]
                          fns = []
                          for qs in range(max(0, r), 4):
                              fl = (i == 0 and qs == max(0, r))
                              fns.append(lambda qs=qs, fl=fl: nc.tensor.matmul(psb[ab][:, qs * 64:(qs + 1) * 64], bft[ai][:, qs * 128:(qs + 1) * 128],
                                                                               vaug[si][:, kt, hh * 64:(hh + 1) * 64],
                                                                               start=fl, stop=(kt == 0), skip_group_check=True))
                          sc.pe_group(fns, reads=[b_bft[ai], b_v[si][kt // 4]], writes=[b_ps[ab]])

                      for i in range(min(LA, n)):
                          s1(0, i)
                          s1(1, i)
                      for i in range(n):
                          s2a(0, i)
                          s2a(1, i)
                          if i + LA < n:
                              s1(0, i + LA)
                              s1(1, i + LA)
                          if i > 0:
                              s3(0, i - 1)
                              s3(1, i - 1)
                          s2b(0, i)
                          s2b(1, i)
                          side.step(3)
                      s3(0, n - 1)
                      s3(1, n - 1)
                      for hh in range(2):
                          ab = abank[hh]
                          sc.op(ACT, lambda: nc.scalar.copy(out=mixt[fi][:, :, hh * 64:(hh + 1) * 64],
                                                            in_=psb[ab][:, 0:256].rearrange("p (q d) -> p q d", q=4)),
                                reads=[b_ps[ab]], writes=[b_mixt[fi]])
                      mix_transpose(fi, 4 + c, qb)

              phase1()
              chk("proj0")
              for g in range(8):
                  si = g % 2
                  if g + 1 < 8:
                      side.steps = project_steps(g + 1, (g + 1) % 2)
                  if g < 4:
                      attn_A(g, si)
                  else:
                      attn_B(g - 4, si)
                  side.flush()
                  chk(f"g{g}")
              sc.barrier()

          with Arena() as ep:
              hblk2 = [sb(ep, f"hblk{j}", [128, 4, D], F32) for j in range(2)]
              hblk2 = [hblk2[0][:], hblk2[1][:],
                       mixT[:, 0:2, :, :].rearrange("p a c n -> p (a c n)").bitcast(F32).rearrange("p (i d) -> p i d", i=4)]
              gT = sb(ep, "gT", [128, 22, TB], BF16)
              wsl2 = sb(ep, "wsl2", [128, 4096], BF16)
              del wsl_all[2:]
              wsl_all.append(wsl2)
              wside = sb(ep, "wside", [128, 4096], BF16)
              wppt = sb(ep, "wppt", [128, 2, D], BF16)
              gbp = [sb(ep, f"gbp{k}", [128, D], F32) for k in range(9)]
              hbf = [sb(ep, f"p_hbf{i}", [128, D], BF16) for i in range(2)]
              stat = [sb(ep, f"p_stat{i}", [128, 20], F32) for i in range(3)]
              p32 = [sb(ep, f"p32_{i}", [128, PD], F32) for i in range(2)]
              pbf = [sb(ep, f"pbf{i}", [128, PD], BF16) for i in range(4)]
              pT = sb(ep, "pT", [128, 2, TB], BF16)
              eg = [sb(ep, f"eg{i}", [128, 512], F32) for i in range(2)]
              b_hblk2 = [[Buf(f"hblk{j}_{i}") for i in range(4)] for j in range(3)]
              b_gT = [Buf(f"gT{f}") for f in range(22)]
              b_hbf = [Buf("p_hbf0"), Buf("p_hbf1")]
              b_stat = [Buf(f"p_stat{i}") for i in range(3)]
              b_p32 = [Buf(f"p32_{i}") for i in range(2)]
              b_pbf = [Buf(f"pbf{i}") for i in range(4)]
              b_pT = [Buf(f"pT{i}") for i in range(4)]
              b_eg = [Buf("eg0"), Buf("eg1")]
              rot_eg = Rot([0, 1])
              rot_w = Rot([4, 5, 6])
              rot_hbf = Rot([0, 1])
              rot_stat = Rot([0, 1, 2])
              for k in range(9):
                  sc.dma(SP, s_gbs[k], gbp[k][:], vecs_d[k, :, :], writes=[b_gbs[k]])
              GB = [(gbp[k][:], b_gbs[k]) for k in range(9)]

              def side_wload(view_shape, src):
                  kc, n = view_shape
                  view = wside[:, 0:kc * n].rearrange("p (k n) -> p k n", k=kc)
                  sc.dma(POOL, s_wside, view, src, reads=b_prep, writes=[b_wside])
                  return view, b_wside

              def ln_pipeline(hb, bh, gw, bw_, tb, with_T, after_b=None):
                  ctx = {}

                  def mk_a(i):
                      def f():
                          si_ = rot_stat.next()
                          hi = rot_hbf.next() if with_T else None
                          ctx[i] = (si_, hi)
                          outs = [(hb[:, i, :], bh[i])] + ([(hbf[hi][:], b_hbf[hi])] if with_T else [])
                          layer_norm(hb[:, i, :], bh[i], gw, bw_, stat[si_], b_stat[si_], outs, gmul=DVE, part="a")
                      return f

                  def mk_b(i):
                      def f():
                          si_, hi = ctx[i]
                          outs = [(hb[:, i, :], bh[i])] + ([(hbf[hi][:], b_hbf[hi])] if with_T else [])
                          layer_norm(hb[:, i, :], bh[i], gw, bw_, stat[si_], b_stat[si_], outs, gmul=DVE, part="b")
                          if after_b is not None:
                              after_b(i)
                      return f

                  def mk_t(i):
                      def f():
                          si_, hi = ctx[i]
                          transpose_to_hT(hbf[hi], b_hbf[hi], 4 * tb + i, hT, b_hT[4 * tb + i])
                      return f
                  a = [mk_a(i) for i in range(4)]
                  bb = [mk_b(i) for i in range(4)]
                  if with_T:
                      t_ = [mk_t(i) for i in range(4)]
                      return [a[0], a[1], bb[0], a[2], bb[1], t_[0], a[3], bb[2], t_[1], bb[3], t_[2], t_[3]]
                  return [a[0], a[1], bb[0], a[2], bb[1], a[3], bb[2], bb[3]]

              def X_steps(tb):
                  j = tb % 3
                  hb = hblk2[j]
                  bh = b_hblk2[j]
                  box = {}
                  steps = []
                  for i in range(4):
                      def st_x(i=i):
                          t = 4 * tb + i
                          sc.dma(POOL, s_hb[j * 4 + i], hb[:, i, :], x_d[b, t * 128:(t + 1) * 128, :], writes=[bh[i]])
                      steps.append(st_x)
                  steps += ln_pipeline(hb, bh, GB[0], GB[1], tb, False)
                  for nh in range(2):
                      def st_w(nh=nh):
                          box["wo"] = side_wload((8, 512), kcview(wout_s[:, nh * 512:(nh + 1) * 512]))
                      steps.append(st_w)
                      for i in range(4):
                          def st_b(i=i, nh=nh):
                              t = 4 * tb + i
                              wv, bw = box["wo"]
                              k = rot_w.next()
                              sc.pe_group([lambda kc=kc: nc.tensor.matmul(psb[k][:, :], mixT[:, tb, kc, i * 128:(i + 1) * 128], wv[:, kc, :],
                                                                          start=(kc == 0), stop=(kc == 7)) for kc in range(8)],
                                          reads=[bw] + [b_mixT[c][tb] for c in range(8)], writes=[b_ps[k]])
                              sc.op(DVE, lambda: nc.vector.scalar_tensor_tensor(
                                  out=hb[:, i, nh * 512:(nh + 1) * 512], in0=hb[:, i, nh * 512:(nh + 1) * 512], scalar=ALPHA,
                                  in1=psb[k][:, :], op0=ALU.mult, op1=ALU.add), reads=[b_ps[k]], writes=[bh[i]])
                          steps.append(st_b)
                  steps += ln_pipeline(hb, bh, GB[2], GB[3], tb, True)
                  return steps

              def E_steps(tb):
                  j = tb % 3
                  hb = hblk2[j]
                  bh = b_hblk2[j]
                  box = {}
                  steps = []
                  ln_steps = ln_pipeline(hb, bh, GB[4], GB[5], tb, True)
                  pls = []
                  pcs = []
                  for i in range(4):
                      def st_p(i=i):
                          t = 4 * tb + i
                          sc.dma(POOL, s_p[i % 2], p32[i % 2][:], p_d[b, t * 128:(t + 1) * 128, :], writes=[b_p32[i % 2]])
                      pls.append(st_p)

                      def st_pc(i=i):
                          sc.op(POOL, lambda: nc.gpsimd.tensor_copy(out=pbf[i][:], in_=p32[i % 2][:]), reads=[b_p32[i % 2]], writes=[b_pbf[i]])
                      pcs.append(st_pc)

                  def st_wpp():
                      sc.dma(POOL, s_wpp, wppt[:], kcview(wpp_s[:, :]), reads=b_prep, writes=[b_wpp])
                  steps += [pls[0], pls[1], st_wpp, ln_steps[0], pcs[0], ln_steps[1], pcs[1], pls[2], ln_steps[2], pls[3], ln_steps[3], pcs[2], ln_steps[4], pcs[3]] + ln_steps[5:]
                  for i in range(4):
                      def st_pt(i=i):
                          sc.pe_group([lambda jj=jj: nc.tensor.transpose(psT[:, jj * 128:(jj + 1) * 128],
                                                                       pbf[i][:, jj * 128:(jj + 1) * 128], ident) for jj in range(2)],
                                      reads=[b_pbf[i], b_c2], writes=[b_psT])
                          sc.op(ACT, lambda: nc.scalar.copy(out=pT[:, :, i * 128:(i + 1) * 128],
                                                            in_=psT[:, 0:256].rearrange("p (c n) -> p c n", c=2)),
                                reads=[], writes=[b_psT, b_pT[i]])
                      steps.append(st_pt)
                  for nh in range(2):
                      def st_w(nh=nh):
                          box["wpg"] = side_wload((8, 512), kcview(wpg_s[:, nh * 512:(nh + 1) * 512]))
                      steps.append(st_w)
                      for i in range(4):
                          def st_g(i=i, nh=nh):
                              t = 4 * tb + i
                              wv, bw = box["wpg"]
                              kG = rot_w.next()
                              kE = rot_w.next()
                              sc.pe_group([lambda kc=kc: nc.tensor.matmul(psb[kG][:, :], hT[:, kc, t * 128:(t + 1) * 128], wv[:, kc, :],
                                                                          start=(kc == 0), stop=(kc == 7)) for kc in range(8)],
                                          reads=[bw, b_hT[t]], writes=[b_ps[kG]])
                              sc.pe_group([lambda kc=kc: nc.tensor.matmul(psb[kE][:, :], pT[:, kc, i * 128:(i + 1) * 128],
                                                                          wppt[:, kc, nh * 512:(nh + 1) * 512], start=(kc == 0), stop=(kc == 1))
                                           for kc in range(2)], reads=[b_wpp, b_pT[i]], writes=[b_ps[kE]])
                              ei = rot_eg.next()
                              sc.op(DVE, lambda: nc.vector.tensor_tensor(out=eg[ei][:], in0=psb[kG][:, :], in1=gbp[8][:, nh * 512:(nh + 1) * 512],
                                                                         op=ALU.add), reads=[b_ps[kG], b_gbs[8]], writes=[b_eg[ei]])
                              sc.op(ACT, lambda: nc.scalar.activation(out=eg[ei][:], in_=eg[ei][:], func=AF.Sigmoid),
                                    reads=[], writes=[b_eg[ei]])
                              sc.op(DVE, lambda: nc.vector.tensor_tensor(out=eg[ei][:], in0=psb[kE][:, :], in1=eg[ei][:], op=ALU.mult),
                                    reads=[b_ps[kE]], writes=[b_eg[ei]])
                              sc.op(DVE, lambda: nc.vector.scalar_tensor_tensor(
                                  out=hb[:, i, nh * 512:(nh + 1) * 512], in0=hb[:, i, nh * 512:(nh + 1) * 512], scalar=ALPHA,
                                  in1=eg[ei][:], op0=ALU.mult, op1=ALU.add), reads=[b_eg[ei]], writes=[bh[i]])
                          steps.append(st_g)
                  def store(i):
                      t = 4 * tb + i
                      sc.dma(POOL, s_outs[j * 4 + i], out_d[b, t * 128:(t + 1) * 128, :], hb[:, i, :], reads=[bh[i]])
                  steps += ln_pipeline(hb, bh, GB[6], GB[7], tb, False, after_b=store)
                  return steps

              class Side2:
                  def __init__(self):
                      self.steps = []

                  def step(self, k=1):
                      for _ in range(k):
                          if self.steps:
                              self.steps.pop(0)()

                  def flush(self):
                      while self.steps:
                          self.steps.pop(0)()
              side2 = Side2()
              side2.steps = X_steps(0)
              side2.flush()
              for tb in range(NB):
                  cols = slice(tb * TB, (tb + 1) * TB)
                  hts = b_hT[4 * tb:4 * tb + 4]
                  hb = hblk2[tb % 3]
                  bh = b_hblk2[tb % 3]
                  es_ = E_steps(tb - 1) if tb > 0 else []
                  xs_ = X_steps(tb + 1) if tb + 1 < NB else []
                  if tb + 1 == 2:
                      for bb_ in b_hblk2[2]:
                          for c_ in range(8):
                              for q_ in range(2):
                                  mb = b_mixT[c_][q_]
                                  for tk in [mb.w] + list(mb.r.values()):
                                      if tk is not None:
                                          cur = bb_.r.get(tk[0].name)
                                          if cur is None or cur[1] < tk[1]:
                                              bb_.r[tk[0].name] = tk
                  while es_ or xs_:
                      if es_:
                          side2.steps.append(es_.pop(0))
                      if xs_:
                          side2.steps.append(xs_.pop(0))
                  fc = 0
                  for n in range(6):
                      ncol = min(512, FF - n * 512)
                      wg, bwg = wload(3, (8, ncol), kcview(wg_s[:, n * 512:n * 512 + ncol]))
                      wu, bwu = wload(3, (8, ncol), kcview(wu_s[:, n * 512:n * 512 + ncol]))
                      for f in range(ncol // 128):
                          kg_, ku_ = (0, 1) if fc % 2 == 0 else (2, 3)
                          sc.pe_group([lambda kc=kc: nc.tensor.matmul(psb[kg_][:, :], wg[:, kc, f * 128:(f + 1) * 128], hT[:, kc, cols],
                                                                      start=(kc == 0), stop=(kc == 7)) for kc in range(8)],
                                      reads=[bwg] + hts, writes=[b_ps[kg_]])
                          sc.pe_group([lambda kc=kc: nc.tensor.matmul(psb[ku_][:, :], wu[:, kc, f * 128:(f + 1) * 128], hT[:, kc, cols],
                                                                      start=(kc == 0), stop=(kc == 7)) for kc in range(8)],
                                      reads=[bwu] + hts, writes=[b_ps[ku_]])
                          ei = rot_eg.next()
                          sc.op(ACT, lambda: nc.scalar.activation(out=eg[ei][:], in_=psb[kg_][:, :], func=AF.Silu),
                                reads=[b_ps[kg_]], writes=[b_eg[ei]])
                          sc.op(DVE, lambda: nc.vector.tensor_tensor(out=gT[:, fc, :], in0=eg[ei][:], in1=psb[ku_][:, :], op=ALU.mult),
                                reads=[b_eg[ei], b_ps[ku_]], writes=[b_gT[fc]])
                          fc += 1
                          side2.step(1)
                  chk("gateup")
                  for nh in range(2):
                      for kg in range(3):
                          nk = min(8, 22 - kg * 8)
                          wd, bwd = wload(3, (nk, 512), kcview(wd_s[kg * 1024:kg * 1024 + nk * 128, nh * 512:(nh + 1) * 512]))
                          for i in range(4):
                              sc.pe_group([lambda kl=kl: nc.tensor.matmul(psb[i][:, :], gT[:, kg * 8 + kl, i * 128:(i + 1) * 128], wd[:, kl, :],
                                                                          start=(kg == 0 and kl == 0), stop=(kg == 2 and kl == nk - 1))
                                           for kl in range(nk)],
                                          reads=[bwd] + b_gT[kg * 8:kg * 8 + nk], writes=[b_ps[i]])
                              side2.step(2)
                      for i in range(4):
                          sc.op(DVE, lambda: nc.vector.scalar_tensor_tensor(
                              out=hb[:, i, nh * 512:(nh + 1) * 512], in0=hb[:, i, nh * 512:(nh + 1) * 512], scalar=ALPHA,
                              in1=psb[i][:, :], op0=ALU.mult, op1=ALU.add), reads=[b_ps[i]], writes=[bh[i]])
                  chk("down")
                  side2.flush()
              side2.steps = E_steps(NB - 1)
              side2.flush()
              sc.barrier(exclude=tuple(s_preps))
              del wsl_all[2:]
              wcount[0] = 0

    except _Stop:
        sc.barrier()
    for so in s_outs:
        if so.count > 0:
            SP.h.wait_ge(so.h, so.count)
    es.close()
    return nc


def _consts():
    c = np.zeros((128, 640), np.float32)
    idx = np.arange(128)
    c[:, 0:128] = np.eye(128, dtype=np.float32)
    c[:, 128:256] = (idx[:, None] > idx[None, :]).astype(np.float32)
    c[:, 256:384] = (idx[:, None] <= idx[None, :]).astype(np.float32)
    pm = np.zeros((128, 128), np.float32)
    for m in range(128):
        if (m % 64) < 32:
            pm[m + 32, m] = -1.0
        else:
            pm[m - 32, m] = 1.0
    c[:, 384:512] = pm
    c[:, 512:640] = (idx[:, None] < idx[None, :]).astype(np.float32)
    inv_freq = (1.0 / (np.float32(10000.0) ** (np.arange(0, 64, 2, dtype=np.float32) / np.float32(64.0)))).astype(np.float32)
    ang = np.arange(S, dtype=np.float32)[:, None] * inv_freq[None, :]
    ang = np.concatenate([ang, ang], axis=-1)
    cs = np.zeros((2, 128, S), np.float32)
    cs[0] = np.tile(np.cos(ang).T, (2, 1))
    cs[1] = np.tile(np.sin(ang).T, (2, 1))
    return c, cs


def make_in_maps(inputs, ncores=NCORES, nseq=NSEQ):
    f = lambda a: np.ascontiguousarray(np.asarray(a, dtype=np.float32))
    c, cs = _consts()
    vecs = np.stack([np.broadcast_to(f(inputs[n]).reshape(1, D), (128, D)) for n in VEC_NAMES]).astype(np.float32)
    small = np.concatenate([f(inputs["lam_q1"]).reshape(-1), f(inputs["lam_k1"]).reshape(-1), f(inputs["lam_q2"]).reshape(-1),
                            f(inputs["lam_k2"]).reshape(-1), f(inputs["subln_g"]).reshape(-1)])
    small = np.ascontiguousarray(np.broadcast_to(small[None, :], (128, 384))).astype(np.float32)
    shared = {
        "w_in": f(inputs["w_in"])[0], "w_out": f(inputs["w_out"])[0], "w_ffn_gate": f(inputs["w_ffn_gate"])[0],
        "w_ffn_up": f(inputs["w_ffn_up"])[0], "w_ffn_down": f(inputs["w_ffn_down"])[0],
        "w_ple_gate": f(inputs["w_ple_gate"])[0], "w_ple_proj": f(inputs["w_ple_proj"])[0],
        "vecs": np.ascontiguousarray(vecs), "small": small, "cbf": c, "cossin": cs,
    }
    x = f(inputs["x"])
    p = f(inputs["p"])[0]
    maps = []
    for i in range(ncores):
        m = dict(shared)
        m["x"] = np.ascontiguousarray(x[i * nseq:(i + 1) * nseq])
        m["p"] = np.ascontiguousarray(p[i * nseq:(i + 1) * nseq])
        maps.append(m)
    return maps


def kernel(**inputs):
    nc = build(NSEQ)
    in_maps = make_in_maps(inputs)
    res = run_bass_kernel_spmd(nc, in_maps, core_ids=list(range(NCORES)))
    return np.concatenate([np.asarray(r["out"], dtype=np.float32) for r in res.results], axis=0)
```

```python
import contextlib
import math
import numpy as np
import concourse.bass as bass
import concourse.mybir as mybir
from concourse.bass_utils import run_bass_kernel_spmd

F32 = mybir.dt.float32
BF16 = mybir.dt.bfloat16
AF = mybir.ActivationFunctionType
ALU = mybir.AluOpType
AX = mybir.AxisListType

NCORES = 8
BATCH = 32
S = 2048
D = 1024
FF = 2816
PD = 256
NSEQ = BATCH // NCORES
NT = S // 128
TB = 512
NB = S // TB
EPS = 1e-5
ALPHA = 2.0 ** 0.25
LAMBDA_INIT = 0.8 - 0.6 * math.exp(0.0)
SCALE = 0.125
PREP_INFLIGHT = 16

VEC_NAMES = ["ln_emb_g", "ln_emb_b", "ln1_g", "ln1_b", "ln2_g", "ln2_b", "ln3_g", "ln3_b", "b_ple_gate"]


class Sem:
    def __init__(self, h, name):
        self.h = h
        self.name = name
        self.count = 0


class Buf:
    __slots__ = ("name", "w", "r", "excl")

    def __init__(self, name, excl=False):
        self.name = name
        self.w = None
        self.r = {}
        self.excl = excl


class Eng:
    def __init__(self, name, h, sem, is_pe=False):
        self.name = name
        self.h = h
        self.sem = sem
        self.is_pe = is_pe
        self.seen = {}


class Sched:
    def __init__(self, nc, es):
        self.nc = nc
        self.es = es
        self.nsem = 0
        self.dma_sems = []
        mk = self.new_sem
        self.PE = Eng("pe", nc.tensor, mk("e_pe"), True)
        self.ACT = Eng("act", nc.scalar, mk("e_act"))
        self.DVE = Eng("dve", nc.vector, mk("e_dve"))
        self.POOL = Eng("pool", nc.gpsimd, mk("e_pool"))
        self.SP = Eng("sp", nc.sync, mk("e_sp"))
        self.engines = [self.PE, self.ACT, self.DVE, self.POOL, self.SP]

    def new_sem(self, name):
        h = self.es.enter_context(self.nc.semaphore(name))
        self.nsem += 1
        return Sem(h, name)

    def new_dma_sem(self, name):
        s = self.new_sem(name)
        self.dma_sems.append(s)
        return s

    def _waits(self, eng, reads, writes):
        need = {}

        def add(t):
            if t is None:
                return
            sem, val = t
            if eng.is_pe and sem is eng.sem:
                return
            cur = need.get(sem.name)
            if cur is None or cur[1] < val:
                need[sem.name] = (sem, val)

        for b in reads:
            add(b.w)
            if b.excl:
                for t in b.r.values():
                    add(t)
        for b in writes:
            add(b.w)
            for t in b.r.values():
                add(t)
        for name, (sem, val) in need.items():
            if eng.seen.get(name, 0) < val:
                eng.h.wait_ge(sem.h, val)
                eng.seen[name] = val

    def _record(self, t, reads, writes):
        sem, val = t
        for b in reads:
            if b.excl:
                b.w = t
                b.r = {}
                continue
            cur = b.r.get(sem.name)
            if cur is None or cur[1] < val:
                b.r[sem.name] = t
        for b in writes:
            b.w = t
            b.r = {}

    def op(self, eng, fn, reads=(), writes=()):
        self._waits(eng, reads, writes)
        ins = fn()
        eng.sem.count += 1
        ins.then_inc(eng.sem.h, 1)
        t = (eng.sem, eng.sem.count)
        self._record(t, reads, writes)
        return t

    def pe_group(self, fns, reads=(), writes=()):
        eng = self.PE
        self._waits(eng, reads, writes)
        ins = None
        for fn in fns:
            ins = fn()
        eng.sem.count += 1
        ins.then_inc(eng.sem.h, 1)
        t = (eng.sem, eng.sem.count)
        self._record(t, reads, writes)
        return t

    def dma(self, q, sem, out, in_, reads=(), writes=()):
        self._waits(q, reads, writes)
        ins = q.h.dma_start(out=out, in_=in_)
        sem.count += 16
        ins.then_inc(sem.h, 16)
        t = (sem, sem.count)
        self._record(t, reads, writes)
        return t

    def barrier(self, exclude=()):
        tickets = [(e.sem, e.sem.count) for e in self.engines if e.sem.count > 0]
        tickets += [(s, s.count) for s in self.dma_sems if s.count > 0 and s not in exclude]
        for e in self.engines:
            for sem, val in tickets:
                if e.is_pe and sem is e.sem:
                    continue
                if e.seen.get(sem.name, 0) < val:
                    e.h.wait_ge(sem.h, val)
                    e.seen[sem.name] = val


class _Stop(Exception):
    pass


class Arena:
    def __enter__(self):
        self.st = contextlib.ExitStack()
        return self.st

    def __exit__(self, et, ev, tb):
        self.st.close()
        return False


def build(nseq=NSEQ, dbg=False, stage=None):
    nc = bass.Bass("TRN2", target_bir_lowering=False)
    dt = nc.dram_tensor
    x_d = dt("x", [nseq, S, D], F32, kind="ExternalInput").ap()
    p_d = dt("p", [nseq, S, PD], F32, kind="ExternalInput").ap()
    w_in_d = dt("w_in", [D, 3072], F32, kind="ExternalInput").ap()
    w_out_d = dt("w_out", [D, D], F32, kind="ExternalInput").ap()
    w_g_d = dt("w_ffn_gate", [D, FF], F32, kind="ExternalInput").ap()
    w_u_d = dt("w_ffn_up", [D, FF], F32, kind="ExternalInput").ap()
    w_d_d = dt("w_ffn_down", [FF, D], F32, kind="ExternalInput").ap()
    w_pg_d = dt("w_ple_gate", [D, D], F32, kind="ExternalInput").ap()
    w_pp_d = dt("w_ple_proj", [PD, D], F32, kind="ExternalInput").ap()
    vecs_d = dt("vecs", [9, 128, D], F32, kind="ExternalInput").ap()
    small_d = dt("small", [128, 384], F32, kind="ExternalInput").ap()
    cbf_d = dt("cbf", [128, 640], F32, kind="ExternalInput").ap()
    cs_d = dt("cossin", [2, 128, S], F32, kind="ExternalInput").ap()
    out_d = dt("out", [nseq, S, D], F32, kind="ExternalOutput").ap()
    win_s = dt("win_s", [8, D, 384], BF16).ap()
    wout_s = dt("wout_s", [D, D], BF16).ap()
    wg_s = dt("wg_s", [D, FF], BF16).ap()
    wu_s = dt("wu_s", [D, FF], BF16).ap()
    wd_s = dt("wd_s", [FF, D], BF16).ap()
    wpg_s = dt("wpg_s", [D, D], BF16).ap()
    wpp_s = dt("wpp_s", [PD, D], BF16).ap()

    es = contextlib.ExitStack()
    sc = Sched(nc, es)
    PE, ACT, DVE, POOL, SP = sc.PE, sc.ACT, sc.DVE, sc.POOL, sc.SP

    uniq = [0]

    def sb(stack, name, shape, dtype):
        uniq[0] += 1
        return stack.enter_context(nc.sbuf_tensor(f"{name}_u{uniq[0]}", shape, dtype))

    cbf = sb(es, "cbf_t", [128, 640], BF16)
    g08 = sb(es, "g08", [128, 128], F32)
    lamw = sb(es, "lamw", [128, 8], F32)
    hT = sb(es, "hT", [128, 8, S], BF16)
    mixT = sb(es, "mixT", [128, NB, 8, TB], BF16)
    wsl = [sb(es, f"wsl{i}", [128, 4096], BF16) for i in range(2)]
    psb = [es.enter_context(nc.psum_tensor(f"psb{i}", [128, 512], F32)) for i in range(7)]
    psT = es.enter_context(nc.psum_tensor("psT", [128, 1024], BF16))
    b_ps = [Buf(f"psb{i}", excl=True) for i in range(7)]
    b_psT = Buf("psT", excl=True)

    ident = cbf[:, 0:128]
    tri = cbf[:, 128:256]
    tric = cbf[:, 256:384]
    perms = cbf[:, 384:512]
    maskb = cbf[:, 512:640]
    b_const = Buf("const")
    b_gbs = [Buf(f"gb{i}") for i in range(9)]
    s_gbs = [sc.new_dma_sem(f"s_gb{i}") for i in range(9)]
    b_cs = Buf("cossin")
    s_cs = sc.new_dma_sem("s_cs")
    b_hT = [Buf(f"hT{t}") for t in range(NT)]
    b_mixT = [[Buf(f"mixT{c}_{q}") for q in range(NB)] for c in range(8)]
    b_wsl = [Buf(f"wsl{i}") for i in range(3)]
    s_wsl = [sc.new_dma_sem(f"s_wsl{i}") for i in range(3)]
    wsl_all = list(wsl)
    b_wside = Buf("wside")
    s_wside = sc.new_dma_sem("s_wside")
    b_wpp = Buf("wppt")
    s_wpp = sc.new_dma_sem("s_wpp")

    s_c = sc.new_dma_sem("s_const")
    e0 = Arena()
    e0s = e0.__enter__()
    cst32 = sb(e0s, "cst32", [128, 640], F32)
    small = sb(e0s, "small_t", [128, 384], F32)
    sc.dma(SP, s_c, cst32[:], cbf_d[:, :], writes=[b_const])
    sc.dma(SP, s_c, small[:], small_d[:, :], writes=[b_const])
    b_c2 = Buf("const2")
    sc.op(DVE, lambda: nc.vector.tensor_copy(out=cbf[:], in_=cst32[:]), reads=[b_const], writes=[b_c2])
    sc.op(DVE, lambda: nc.vector.tensor_scalar(out=g08[:], in0=small[:, 256:384], scalar1=1.0 - LAMBDA_INIT,
                                               scalar2=None, op0=ALU.mult), reads=[b_const], writes=[b_c2])
    b_l = Buf("lam")
    sc.op(DVE, lambda: nc.vector.tensor_tensor(out=cst32[:, 0:64], in0=small[:, 0:64], in1=small[:, 64:128],
                                               op=ALU.mult), reads=[b_c2, b_const], writes=[b_l])
    sc.op(DVE, lambda: nc.vector.tensor_tensor(out=cst32[:, 64:128], in0=small[:, 128:192], in1=small[:, 192:256],
                                               op=ALU.mult), reads=[b_l], writes=[b_l])
    sc.op(DVE, lambda: nc.vector.reduce_sum(out=lamw[:, 0:1], in_=cst32[:, 0:64], axis=AX.X), reads=[b_l], writes=[b_l])
    sc.op(DVE, lambda: nc.vector.reduce_sum(out=lamw[:, 1:2], in_=cst32[:, 64:128], axis=AX.X), reads=[b_l], writes=[b_l])
    sc.op(ACT, lambda: nc.scalar.activation(out=lamw[:, 2:4], in_=lamw[:, 0:2], func=AF.Exp), reads=[b_l], writes=[b_l])
    sc.op(DVE, lambda: nc.vector.tensor_tensor(out=lamw[:, 4:5], in0=lamw[:, 3:4], in1=lamw[:, 2:3], op=ALU.subtract),
          reads=[b_l], writes=[b_l])
    sc.op(DVE, lambda: nc.vector.tensor_scalar(out=lamw[:, 4:5], in0=lamw[:, 4:5], scalar1=-LAMBDA_INIT, scalar2=None,
                                               op0=ALU.add), reads=[b_l], writes=[b_l])
    neglam = lamw[:, 4:5]
    sc.barrier(exclude=())
    e0.__exit__(None, None, None)

    s_preps = [sc.new_dma_sem(f"s_prep{i}") for i in range(PREP_INFLIGHT)]
    b_prep = [Buf(f"wprep{i}") for i in range(PREP_INFLIGHT)]
    nprep = [0]

    def prep(dst, src):
        if stage == "noprep":
            return
        j = nprep[0] % PREP_INFLIGHT
        nprep[0] += 1
        sp_ = s_preps[j]
        if sp_.count > 0 and POOL.seen.get(sp_.name, 0) < sp_.count:
            POOL.h.wait_ge(sp_.h, sp_.count)
            POOL.seen[sp_.name] = sp_.count
        sc.dma(POOL, sp_, dst, src, writes=[b_prep[j]])

    qkv_off = []
    for h in range(4):
        qkv_off.append((128 * h, 512 + 128 * h, 1024 + 128 * h))
    for c in range(4):
        qkv_off.append((1536 + 128 * c, 2048 + 128 * c, 2560 + 128 * c))
    for g in range(8):
        for j in range(3):
            o = qkv_off[g][j]
            prep(win_s[g, :, j * 128:(j + 1) * 128], w_in_d[:, o:o + 128])
    late_preps = []

    def late(dst, src):
        late_preps.append(lambda: prep(dst, src))
    for r in range(0, D, 256):
        late(wout_s[r:r + 256, :], w_out_d[r:r + 256, :])
    for r in range(0, D, 128):
        late(wg_s[r:r + 128, :], w_g_d[r:r + 128, :])
        late(wu_s[r:r + 128, :], w_u_d[r:r + 128, :])
    for r in range(0, FF, 256):
        late(wd_s[r:r + 256, :], w_d_d[r:r + 256, :])
    for r in range(0, D, 256):
        late(wpg_s[r:r + 256, :], w_pg_d[r:r + 256, :])
    late(wpp_s[:, :], w_pp_d[:, :])

    wcount = [0]

    def wload(nslots, view_shape, src):
        i = wcount[0] % nslots
        wcount[0] += 1
        flat = wsl_all[i]
        kc, n = view_shape
        view = flat[:, 0:kc * n].rearrange("p (k n) -> p k n", k=kc)
        sc.dma(SP, s_wsl[i], view, src, reads=b_prep, writes=[b_wsl[i]])
        return view, b_wsl[i]

    def kcview(dram2d):
        return dram2d.rearrange("(k p) n -> p k n", p=128)

    class Rot:
        def __init__(self, items):
            self.items = items
            self.i = 0

        def next(self):
            it = self.items[self.i % len(self.items)]
            self.i += 1
            return it

    def layer_norm(src, b_src, gw, bw_, stat, b_stat, outs, gmul=None, part=None):
        gmul = gmul or POOL
        st = stat[:, 0:12].rearrange("p (a b) -> p a b", a=2)
        mv = stat[:, 12:14]
        lnv = stat[:, 14:15]
        rstd = stat[:, 15:16]
        nmr = stat[:, 16:17]
        if part == "b":
            return _ln_b(src, b_src, gw, bw_, outs, gmul)
        sc.op(DVE, lambda: nc.vector.bn_stats(out=st[:, 0, :], in_=src[:, 0:512]), reads=[b_src], writes=[b_stat])
        sc.op(DVE, lambda: nc.vector.bn_stats(out=st[:, 1, :], in_=src[:, 512:1024]), reads=[b_src, b_stat], writes=[b_stat])
        sc.op(DVE, lambda: nc.vector.bn_aggr(out=mv, in_=stat[:, 0:12]), reads=[b_stat], writes=[b_stat])
        sc.op(ACT, lambda: nc.scalar.activation(out=lnv, in_=stat[:, 13:14], func=AF.Ln, bias=EPS), reads=[b_stat], writes=[b_stat])
        sc.op(ACT, lambda: nc.scalar.activation(out=rstd, in_=lnv, func=AF.Exp, scale=-0.5), reads=[b_stat], writes=[b_stat])
        sc.op(DVE, lambda: nc.vector.scalar_tensor_tensor(out=nmr, in0=stat[:, 12:13], scalar=-1.0, in1=rstd,
                                                          op0=ALU.mult, op1=ALU.mult), reads=[b_stat], writes=[b_stat])
        sc.op(ACT, lambda: nc.scalar.activation(out=src, in_=src, func=AF.Identity, bias=nmr, scale=rstd),
              reads=[b_stat], writes=[b_src])
        if part == "a":
            return
        _ln_b(src, b_src, gw, bw_, outs, gmul)

    def _ln_b(src, b_src, gw, bw_, outs, gmul):
        g_ap, g_buf = gw
        b_ap, b_buf = bw_
        sc.op(gmul, lambda: gmul.h.tensor_tensor(out=src, in0=src, in1=g_ap, op=ALU.mult),
              reads=[g_buf], writes=[b_src])
        o0, b0 = outs[0]
        if b0 is b_src:
            sc.op(DVE, lambda: nc.vector.tensor_tensor(out=o0, in0=src, in1=b_ap, op=ALU.add),
                  reads=[b_buf], writes=[b0])
        else:
            sc.op(DVE, lambda: nc.vector.tensor_tensor(out=o0, in0=src, in1=b_ap, op=ALU.add),
                  reads=[b_src, b_buf], writes=[b0])
        for o, b in outs[1:]:
            sc.op(ACT, lambda: nc.scalar.copy(out=o, in_=o0), reads=[b0], writes=[b])

    def transpose_to_hT(hbf, b_hbf, t, dst, b_dst):
        sc.pe_group([lambda c=c: nc.tensor.transpose(psT[:, c * 128:(c + 1) * 128], hbf[:, c * 128:(c + 1) * 128], ident)
                     for c in range(8)], reads=[b_hbf, b_c2], writes=[b_psT])
        sc.op(ACT, lambda: nc.scalar.copy(out=dst[:, :, t * 128:(t + 1) * 128],
                                          in_=psT[:, :].rearrange("p (c n) -> p c n", c=8)),
              reads=[], writes=[b_psT, b_dst])

    s_outs = [sc.new_dma_sem(f"s_out{i}") for i in range(12)]
    s_x = [sc.new_dma_sem("s_x0"), sc.new_dma_sem("s_x1")]
    s_p = [sc.new_dma_sem(f"s_p{i}") for i in range(4)]
    s_hb = [sc.new_dma_sem(f"s_hb{i}") for i in range(12)]
    dbg_outs = {}

    def chk(name):
        if stage == name:
            raise _Stop()

    try:
      chk("setup")
      chk("noprep")
      for b in range(nseq):
          with Arena() as ea:
              xt = [sb(ea, f"xt{i}", [128, D], F32) for i in range(2)]
              cosT = sb(ea, "cosT", [128, S], F32)
              sinT = sb(ea, "sinT", [128, S], F32)
              gba = [sb(ea, f"gba{i}", [128, D], F32) for i in range(2)]
              sc.dma(SP, s_cs, cosT[:], cs_d[0, :, :], writes=[b_cs])
              sc.dma(SP, s_cs, sinT[:], cs_d[1, :, :], writes=[b_cs])
              for k in range(2):
                  sc.dma(SP, s_gbs[k], gba[k][:], vecs_d[k, :, :], writes=[b_gbs[k]])
              hbf = [sb(ea, f"a_hbf{i}", [128, D], BF16) for i in range(2)]
              stat = [sb(ea, f"a_stat{i}", [128, 20], F32) for i in range(2)]
              qT = [sb(ea, f"qT{i}", [128, S], BF16) for i in range(2)]
              kT = [sb(ea, f"kT{i}", [128, S], BF16) for i in range(2)]
              vaug = [sb(ea, f"vaug{i}", [128, NT, 129], BF16) for i in range(2)]
              bft = [sb(ea, f"bft{i}", [128, 512], BF16) for i in range(12)]
              e32 = [sb(ea, f"e32_{i}", [128, 512], F32) for i in range(5)]
              E32 = [sb(ea, f"E32_{i}", [128, 512], F32) for i in range(6)]
              qb16 = [sb(ea, f"qb16_{i}", [128, 512], BF16) for i in range(2)]
              rt1 = [sb(ea, f"rt1_{i}", [128, 512], F32) for i in range(2)]
              rt2 = sb(ea, "rt2", [128, 512], F32)
              fin = [sb(ea, f"fin{i}", [128, 512], F32) for i in range(2)]
              mixt = [sb(ea, f"mixt{i}", [128, 4, 128], BF16) for i in range(2)]

              b_xt = [Buf("xt0"), Buf("xt1")]
              b_hbf = [Buf("hbf0"), Buf("hbf1")]
              b_stat = [Buf("stat0"), Buf("stat1")]
              b_qT = [[Buf(f"qT{i}_{q}") for q in range(NB)] for i in range(2)]
              b_kT = [[Buf(f"kT{i}_{q}") for q in range(NB)] for i in range(2)]
              b_v = [[Buf(f"v{i}_{q}") for q in range(NB)] for i in range(2)]
              b_bft = [Buf(f"bft{i}") for i in range(12)]
              b_e32 = [Buf(f"e32_{i}") for i in range(5)]
              b_E32 = [Buf(f"E32_{i}") for i in range(6)]
              b_qb16 = [Buf("qb16_0"), Buf("qb16_1")]
              b_rt1 = [Buf("rt1_0"), Buf("rt1_1")]
              b_rt2 = Buf("rt2")
              b_fo = [Buf(f"fo{i}") for i in range(4)]
              b_fsm = [Buf(f"fsm{i}") for i in range(4)]
              b_fjunk = Buf("fjunk")
              b_mixt = [Buf("mixt0"), Buf("mixt1")]
              rot_bft = Rot(list(range(12)))
              rot_e32 = Rot(list(range(5)))
              rot_E32 = Rot(list(range(6)))
              rot_T = Rot([0, 1])

              b_ones = Buf("ones")
              for i in range(2):
                  sc.op(POOL, lambda i=i: nc.gpsimd.memset(vaug[i][:, :, 128:129], 1.0), writes=[b_ones])

              GA = ((gba[0][:], b_gbs[0]), (gba[1][:], b_gbs[1]))

              def p1_a(t):
                  i = t % 2
                  sc.dma(SP, s_x[i], xt[i][:], x_d[b, t * 128:(t + 1) * 128, :], writes=[b_xt[i]])
                  layer_norm(xt[i][:], b_xt[i], GA[0], GA[1], stat[i], b_stat[i], [(hbf[i][:], b_hbf[i])], gmul=DVE, part="a")

              def p1_b(t):
                  i = t % 2
                  layer_norm(xt[i][:], b_xt[i], GA[0], GA[1], stat[i], b_stat[i], [(hbf[i][:], b_hbf[i])], gmul=DVE, part="b")

              def p1_c(t):
                  i = t % 2
                  transpose_to_hT(hbf[i], b_hbf[i], t, hT, b_hT[t])

              def phase1():
                  p0 = project_steps(0, 0)
                  p0.pop(0)()
                  p1_a(0)
                  for t in range(NT):
                      if t + 1 < NT:
                          p1_a(t + 1)
                      p1_b(t)
                      if t > 0:
                          p1_c(t - 1)
                          if (t - 1) % 4 == 3:
                              for _ in range(5):
                                  p0.pop(0)()
                  p1_c(NT - 1)
                  while p0:
                      p0.pop(0)()

              PJ = 6

              def project_steps(g, si):
                  isA = g < 4
                  steps = []
                  box = {}

                  def load():
                      box["w"] = wload(2, (8, 384), kcview(win_s[g]))
                  steps.append(load)
                  for tb in range(NB):
                      hts = b_hT[4 * tb:4 * tb + 4]
                      cols = slice(tb * TB, (tb + 1) * TB)
                      for which in range(2):
                          dstT = (qT, kT)[which][si]
                          bdst = (b_qT, b_kT)[which][si][tb]

                          def qk_step(which=which, dstT=dstT, bdst=bdst, cols=cols, hts=hts):
                              wv, bw = box["w"]
                              ps = psb[PJ]
                              sc.pe_group([lambda kc=kc: nc.tensor.matmul(
                                  ps[:, :], wv[:, kc, which * 128:(which + 1) * 128], hT[:, kc, cols],
                                  start=(kc == 0), stop=(kc == 7)) for kc in range(8)],
                                  reads=[bw] + hts, writes=[b_ps[PJ]])
                              if isA:
                                  r = which
                                  sc.op(ACT, lambda: nc.scalar.copy(out=qb16[r][:], in_=ps[:, :]), reads=[b_ps[PJ]], writes=[b_qb16[r]])
                                  sc.op(DVE, lambda: nc.vector.tensor_tensor(out=rt1[r][:], in0=ps[:, :], in1=cosT[:, cols], op=ALU.mult),
                                        reads=[b_ps[PJ], b_cs], writes=[b_rt1[r]])
                              else:
                                  sc.op(ACT, lambda: nc.scalar.copy(out=dstT[:, cols], in_=ps[:, :]), reads=[b_ps[PJ]], writes=[bdst])
                          steps.append(qk_step)

                          def rope_step(which=which, dstT=dstT, bdst=bdst, cols=cols):
                              ps = psb[PJ]
                              r = which
                              sc.pe_group([lambda: nc.tensor.matmul(ps[:, :], perms, qb16[r][:], start=True, stop=True)],
                                          reads=[b_qb16[r], b_c2], writes=[b_ps[PJ]])
                              sc.op(DVE, lambda: nc.vector.tensor_tensor(out=rt2[:], in0=ps[:, :], in1=sinT[:, cols], op=ALU.mult),
                                    reads=[b_ps[PJ], b_cs], writes=[b_rt2])
                              sc.op(DVE, lambda: nc.vector.tensor_tensor(out=dstT[:, cols], in0=rt1[r][:], in1=rt2[:], op=ALU.add),
                                    reads=[b_rt1[r], b_rt2], writes=[bdst])
                          if isA:
                              steps.append(rope_step)

                      def v_step(tb=tb, hts=hts):
                          wv, bw = box["w"]
                          ps = psb[PJ]
                          fns = []
                          for i4 in range(4):
                              t = 4 * tb + i4
                              for kc in range(8):
                                  fns.append(lambda kc=kc, i4=i4, t=t: nc.tensor.matmul(
                                      ps[:, i4 * 128:(i4 + 1) * 128], hT[:, kc, t * 128:(t + 1) * 128], wv[:, kc, 256:384],
                                      start=(kc == 0), stop=(kc == 7)))
                          sc.pe_group(fns, reads=[bw] + hts, writes=[b_ps[PJ]])
                          sc.op(ACT, lambda: nc.scalar.copy(out=vaug[si][:, 4 * tb:4 * tb + 4, 0:128],
                                                            in_=ps[:, :].rearrange("p (a n) -> p a n", a=4)),
                                reads=[b_ps[PJ]], writes=[b_v[si][tb]])
                      steps.append(v_step)
                  return steps

              class Side:
                  def __init__(self):
                      self.steps = []
                      self.tick = 0

                  def step(self, every):
                      self.tick += 1
                      if self.steps and self.tick % every == 0:
                          self.steps.pop(0)()

                  def flush(self):
                      while self.steps:
                          self.steps.pop(0)()
              side = Side()

              def mix_transpose(mi, chunk, qb):
                  sc.pe_group([lambda j=j: nc.tensor.transpose(psT[:, j * 128:(j + 1) * 128], mixt[mi][:, j, :], ident) for j in range(4)],
                              reads=[b_mixt[mi], b_c2], writes=[b_psT])
                  sc.op(ACT, lambda: nc.scalar.copy(out=mixT[:, qb, chunk, :], in_=psT[:, 0:512]),
                        reads=[], writes=[b_psT, b_mixT[chunk][qb]])

              fin_i = [0]

              def attn_A(h, si):
                  accbanks = [3, 4, 5]
                  rot_sc = Rot([0, 1, 2])

                  def acc(m, qs):
                      idx = m * 4 + qs
                      return psb[accbanks[idx // 3]][:, (idx % 3) * 129:(idx % 3) * 129 + 129]

                  b_acc = [b_ps[k] for k in accbanks]
                  for qb in range(NB):
                      tasks = [(m, kt) for m in range(2) for kt in range(4 * qb + 4)]
                      st = {}
                      started = set()

                      def first_in_bank(m, qs):
                          bank = accbanks[(m * 4 + qs) // 3]
                          if bank in started:
                              return False
                          started.add(bank)
                          return True

                      def qk(i):
                          m, kt = tasks[i]
                          r = kt - 4 * qb
                          c0 = max(0, r) * 128
                          bk = rot_sc.next()
                          pi = rot_bft.next()
                          st[i] = (bk, pi, c0, r)
                          pr = slice(m * 64, (m + 1) * 64)
                          sc.pe_group([lambda: nc.tensor.matmul(psb[bk][:, c0:512], kT[si][pr, kt * 128:(kt + 1) * 128],
                                                                qT[si][pr, qb * TB + c0:(qb + 1) * TB], start=True, stop=True)],
                                      reads=[b_kT[si][kt // 4], b_qT[si][qb]], writes=[b_ps[bk]])

                      def rest(i):
                          m, kt = tasks[i]
                          bk, pi, c0, r = st.pop(i)
                          P = bft[pi]
                          sc.op(ACT, lambda: nc.scalar.activation(out=P[:, c0:512], in_=psb[bk][:, c0:512], func=AF.Exp, scale=SCALE),
                                reads=[b_ps[bk]], writes=[b_bft[pi]])
                          if r >= 0:
                              sc.op(POOL, lambda: nc.gpsimd.memset(P[64:128, c0:c0 + 64], 0.0), writes=[b_bft[pi]])
                          fns = []
                          for qs in range(max(0, r), 4):
                              fl = first_in_bank(m, qs)
                              fns.append(lambda qs=qs, fl=fl: nc.tensor.matmul(acc(m, qs), P[:, qs * 128:(qs + 1) * 128], vaug[si][:, kt, 0:129],
                                                                               start=fl, stop=(kt == 4 * qb + qs), skip_group_check=True))
                          sc.pe_group(fns, reads=[b_bft[pi], b_v[si][kt // 4], b_ones], writes=b_acc)

                      n = len(tasks)
                      qk(0)
                      if n > 1:
                          qk(1)
                      for i in range(n):
                          rest(i)
                          if i + 2 < n:
                              qk(i + 2)
                          side.step(3)
                      fi = fin_i[0] % 2
                      fin_i[0] += 1
                      for qs in range(4):
                          a0 = acc(0, qs)
                          a1 = acc(1, qs)
                          o = fin[0][:, qs * 128:(qs + 1) * 128]
                          sm = fin[1][:, 128 + 8 * qs:136 + 8 * qs]
                          sc.op(DVE, lambda: nc.vector.reciprocal(out=sm[:, 0:1], in_=a0[:, 128:129]), reads=b_acc, writes=[b_fsm[qs]])
                          sc.op(DVE, lambda: nc.vector.reciprocal(out=sm[:, 1:2], in_=a1[:, 128:129]), reads=b_acc, writes=[b_fsm[qs]])
                          sc.op(DVE, lambda: nc.vector.tensor_tensor(out=sm[:, 2:3], in0=sm[:, 1:2], in1=neglam, op=ALU.mult),
                                reads=[b_l], writes=[b_fsm[qs]])
                          sc.op(DVE, lambda: nc.vector.tensor_scalar(out=o, in0=a1[:, 0:128], scalar1=sm[:, 2:3], scalar2=None,
                                                                     op0=ALU.mult), reads=b_acc + [b_fsm[qs]], writes=[b_fo[qs]])
                          sc.op(DVE, lambda: nc.vector.scalar_tensor_tensor(out=o, in0=a0[:, 0:128], scalar=sm[:, 0:1],
                                                                            in1=o, op0=ALU.mult, op1=ALU.add),
                                reads=b_acc + [b_fsm[qs]], writes=[b_fo[qs]])
                      for qs in range(4):
                          o = fin[0][:, qs * 128:(qs + 1) * 128]
                          sm = fin[1][:, 128 + 8 * qs:136 + 8 * qs]
                          sc.op(ACT, lambda: nc.scalar.activation(out=fin[1][:, 0:128], in_=o, func=AF.Square,
                                                                  accum_out=sm[:, 3:4]), reads=[b_fo[qs]], writes=[b_fjunk, b_fsm[qs]])
                          sc.op(ACT, lambda: nc.scalar.activation(out=sm[:, 4:5], in_=sm[:, 3:4], func=AF.Ln, bias=EPS, scale=1.0 / 128.0),
                                reads=[], writes=[b_fsm[qs]])
                          sc.op(ACT, lambda: nc.scalar.activation(out=sm[:, 5:6], in_=sm[:, 4:5], func=AF.Exp, scale=-0.5),
                                reads=[], writes=[b_fsm[qs]])
                      for qs in range(4):
                          o = fin[0][:, qs * 128:(qs + 1) * 128]
                          sm = fin[1][:, 128 + 8 * qs:136 + 8 * qs]
                          sc.op(DVE, lambda: nc.vector.scalar_tensor_tensor(out=mixt[fi][:, qs, :], in0=o, scalar=sm[:, 5:6],
                                                                            in1=g08[:], op0=ALU.mult, op1=ALU.mult),
                                reads=[b_fo[qs], b_fsm[qs], b_c2], writes=[b_mixt[fi]])
                      mix_transpose(fi, h, qb)

              def attn_B(c, si):
                  zbank = [0, 1]
                  sbank = [2, 3]
                  abank = [4, 5]
                  LA = 2
                  for qb in range(NB):
                      fi = fin_i[0] % 2
                      fin_i[0] += 1
                      kts = list(range(4 * qb + 3, -1, -1))
                      n = len(kts)
                      st = {}
                      st3 = {}

                      def s1(hh, i):
                          kt = kts[i]
                          pr = slice(hh * 64, (hh + 1) * 64)
                          r = kt - 4 * qb
                          c0 = max(0, r) * 128
                          bk = zbank[hh]
                          ei = rot_e32.next()
                          spi = rot_bft.next()
                          ti = rot_E32.next()
                          st[(hh, i)] = (ei, spi, ti, c0, r)
                          sc.pe_group([lambda: nc.tensor.matmul(psb[bk][:, c0:512], kT[si][pr, kt * 128:(kt + 1) * 128],
                                                                qT[si][pr, qb * TB + c0:(qb + 1) * TB], start=True, stop=True)],
                                      reads=[b_kT[si][kt // 4], b_qT[si][qb]], writes=[b_ps[bk]])
                          sc.op(ACT, lambda: nc.scalar.activation(out=e32[ei][:, c0:512], in_=psb[bk][:, c0:512], func=AF.Exp, scale=SCALE),
                                reads=[b_ps[bk]], writes=[b_e32[ei]])
                          sc.op(ACT, lambda: nc.scalar.activation(out=e32[ei][:, c0:512], in_=e32[ei][:, c0:512], func=AF.Ln, bias=1.0),
                                reads=[], writes=[b_e32[ei]])
                          spb = bft[spi]
                          sc.op(DVE, lambda: nc.vector.tensor_copy(out=spb[:, c0:512], in_=e32[ei][:, c0:512]),
                                reads=[b_e32[ei]], writes=[b_bft[spi]])
                          if r >= 0:
                              sc.op(POOL, lambda: nc.gpsimd.tensor_tensor(out=spb[:, c0:c0 + 128], in0=spb[:, c0:c0 + 128],
                                                                          in1=maskb, op=ALU.mult),
                                    reads=[b_c2], writes=[b_bft[spi]])
                              if c0 > 0:
                                  sc.op(POOL, lambda: nc.gpsimd.memset(spb[:, 0:c0], 0.0), writes=[b_bft[spi]])
                          sc.op(DVE, lambda: nc.vector.scalar_tensor_tensor(out=E32[ti][:, c0:512], in0=psb[bk][:, c0:512], scalar=SCALE,
                                                                            in1=e32[ei][:, c0:512], op0=ALU.mult, op1=ALU.subtract),
                                reads=[b_ps[bk], b_e32[ei]], writes=[b_E32[ti]])

                      def s2a(hh, i):
                          ei, spi, ti, c0, r = st[(hh, i)]
                          sb_ = sbank[hh]
                          sc.pe_group([lambda: nc.tensor.matmul(psb[sb_][:, :], tri, bft[spi][:, :], start=(i == 0), stop=True,
                                                                skip_group_check=True)],
                                      reads=[b_bft[spi], b_c2], writes=[b_ps[sb_]])
                          sc.op(DVE, lambda: nc.vector.tensor_tensor(out=E32[ti][:, c0:512], in0=E32[ti][:, c0:512], in1=psb[sb_][:, c0:512],
                                                                     op=ALU.subtract),
                                reads=[b_ps[sb_]], writes=[b_E32[ti]])

                      def s2b(hh, i):
                          kt = kts[i]
                          ei, spi, ti, c0, r = st.pop((hh, i))
                          sb_ = sbank[hh]
                          ab = abank[hh]
                          sc.pe_group([lambda: nc.tensor.matmul(psb[sb_][:, :], tric, bft[spi][:, :], start=False, stop=True,
                                                                skip_group_check=True)],
                                      reads=[b_bft[spi], b_c2], writes=[b_ps[sb_]])
                          ai = rot_bft.next()
                          sc.op(ACT, lambda: nc.scalar.activation(out=bft[ai][:, c0:512], in_=E32[ti][:, c0:512], func=AF.Exp),
                                reads=[b_E32[ti]], writes=[b_bft[ai]])
                          if r >= 0:
                              sc.op(POOL, lambda: nc.gpsimd.tensor_tensor(out=bft[ai][:, c0:c0 + 128], in0=bft[ai][:, c0:c0 + 128],
                                                                          in1=maskb, op=ALU.mult),
                                    reads=[b_c2], writes=[b_bft[ai]])
                          st3[(hh, i)] = (ai, r)

                      def s3(hh, i):
                          kt = kts[i]
                          ai, r = st3.pop((hh, i))
                          ab = abank[hh]
                          fns = []
                          for qs in range(max(0, r), 4):
                              fl = (i == 0 and qs == max(0, r))
                              fns.append(lambda qs=qs, fl=fl: nc.tensor.matmul(psb[ab][:, qs * 64:(qs + 1) * 64], bft[ai][:, qs * 128:(qs + 1) * 128],
                                                                               vaug[si][:, kt, hh * 64:(hh + 1) * 64],
                                                                               start=fl, stop=(kt == 0), skip_group_check=True))
                          sc.pe_group(fns, reads=[b_bft[ai], b_v[si][kt // 4]], writes=[b_ps[ab]])

                      for i in range(min(LA, n)):
                          s1(0, i)
                          s1(1, i)
                      for i in range(n):
                          s2a(0, i)
                          s2a(1, i)
                          if i + LA < n:
                              s1(0, i + LA)
                              s1(1, i + LA)
                          if i > 0:
                              s3(0, i - 1)
                              s3(1, i - 1)
                          s2b(0, i)
                          s2b(1, i)
                          side.step(3)
                      s3(0, n - 1)
                      s3(1, n - 1)
                      for hh in range(2):
                          ab = abank[hh]
                          sc.op(ACT, lambda: nc.scalar.copy(out=mixt[fi][:, :, hh * 64:(hh + 1) * 64],
                                                            in_=psb[ab][:, 0:256].rearrange("p (q d) -> p q d", q=4)),
                                reads=[b_ps[ab]], writes=[b_mixt[fi]])
                      mix_transpose(fi, 4 + c, qb)

              phase1()
              chk("proj0")
              for g in range(8):
                  si = g % 2
                  if g + 1 < 8:
                      side.steps = project_steps(g + 1, (g + 1) % 2)
                  if late_preps:
                      take = [late_preps.pop(0) for _ in range(min(6, len(late_preps)))]
                      merged = []
                      ps_ = list(side.steps)
                      while ps_ or take:
                          if take:
                              merged.append(take.pop(0))
                          for _ in range(3):
                              if ps_:
                                  merged.append(ps_.pop(0))
                      side.steps = merged
                  if g < 4:
                      attn_A(g, si)
                  else:
                      attn_B(g - 4, si)
                  side.flush()
                  chk(f"g{g}")
              while late_preps:
                  late_preps.pop(0)()
              sc.barrier()

          with Arena() as ep:
              hblk2 = [sb(ep, f"hblk{j}", [128, 4, D], F32) for j in range(2)]
              hblk2 = [hblk2[0][:], hblk2[1][:],
                       mixT[:, 0:2, :, :].rearrange("p a c n -> p (a c n)").bitcast(F32).rearrange("p (i d) -> p i d", i=4)]
              gT = sb(ep, "gT", [128, 22, TB], BF16)
              wsl2 = sb(ep, "wsl2", [128, 4096], BF16)
              del wsl_all[2:]
              wsl_all.append(wsl2)
              wside = sb(ep, "wside", [128, 4096], BF16)
              wppt = sb(ep, "wppt", [128, 2, D], BF16)
              gbp = [sb(ep, f"gbp{k}", [128, D], F32) for k in range(9)]
              hbf = [sb(ep, f"p_hbf{i}", [128, D], BF16) for i in range(2)]
              stat = [sb(ep, f"p_stat{i}", [128, 20], F32) for i in range(3)]
              p32 = [sb(ep, f"p32_{i}", [128, PD], F32) for i in range(2)]
              pbf = [sb(ep, f"pbf{i}", [128, PD], BF16) for i in range(4)]
              pT = sb(ep, "pT", [128, 2, TB], BF16)
              eg = [sb(ep, f"eg{i}", [128, 512], F32) for i in range(2)]
              b_hblk2 = [[Buf(f"hblk{j}_{i}") for i in range(4)] for j in range(3)]
              b_gT = [Buf(f"gT{f}") for f in range(22)]
              b_hbf = [Buf("p_hbf0"), Buf("p_hbf1")]
              b_stat = [Buf(f"p_stat{i}") for i in range(3)]
              b_p32 = [Buf(f"p32_{i}") for i in range(2)]
              b_pbf = [Buf(f"pbf{i}") for i in range(4)]
              b_pT = [Buf(f"pT{i}") for i in range(4)]
              b_eg = [Buf("eg0"), Buf("eg1")]
              rot_eg = Rot([0, 1])
              rot_w = Rot([4, 5, 6])
              rot_hbf = Rot([0, 1])
              rot_stat = Rot([0, 1, 2])
              for k in range(9):
                  sc.dma(SP, s_gbs[k], gbp[k][:], vecs_d[k, :, :], writes=[b_gbs[k]])
              GB = [(gbp[k][:], b_gbs[k]) for k in range(9)]

              def side_wload(view_shape, src):
                  kc, n = view_shape
                  view = wside[:, 0:kc * n].rearrange("p (k n) -> p k n", k=kc)
                  sc.dma(POOL, s_wside, view, src, reads=b_prep, writes=[b_wside])
                  return view, b_wside

              def ln_pipeline(hb, bh, gw, bw_, tb, with_T, after_b=None):
                  ctx = {}

                  def mk_a(i):
                      def f():
                          si_ = rot_stat.next()
                          hi = rot_hbf.next() if with_T else None
                          ctx[i] = (si_, hi)
                          outs = [(hb[:, i, :], bh[i])] + ([(hbf[hi][:], b_hbf[hi])] if with_T else [])
                          layer_norm(hb[:, i, :], bh[i], gw, bw_, stat[si_], b_stat[si_], outs, gmul=DVE, part="a")
                      return f

                  def mk_b(i):
                      def f():
                          si_, hi = ctx[i]
                          outs = [(hb[:, i, :], bh[i])] + ([(hbf[hi][:], b_hbf[hi])] if with_T else [])
                          layer_norm(hb[:, i, :], bh[i], gw, bw_, stat[si_], b_stat[si_], outs, gmul=DVE, part="b")
                          if after_b is not None:
                              after_b(i)
                      return f

                  def mk_t(i):
                      def f():
                          si_, hi = ctx[i]
                          transpose_to_hT(hbf[hi], b_hbf[hi], 4 * tb + i, hT, b_hT[4 * tb + i])
                      return f
                  a = [mk_a(i) for i in range(4)]
                  bb = [mk_b(i) for i in range(4)]
                  if with_T:
                      t_ = [mk_t(i) for i in range(4)]
                      return [a[0], a[1], bb[0], a[2], bb[1], t_[0], a[3], bb[2], t_[1], bb[3], t_[2], t_[3]]
                  return [a[0], a[1], bb[0], a[2], bb[1], a[3], bb[2], bb[3]]

              def X_steps(tb):
                  j = tb % 3
                  hb = hblk2[j]
                  bh = b_hblk2[j]
                  box = {}
                  steps = []
                  for i in range(4):
                      def st_x(i=i):
                          t = 4 * tb + i
                          sc.dma(POOL, s_hb[j * 4 + i], hb[:, i, :], x_d[b, t * 128:(t + 1) * 128, :], writes=[bh[i]])
                      steps.append(st_x)
                  steps += ln_pipeline(hb, bh, GB[0], GB[1], tb, False)
                  for nh in range(2):
                      def st_w(nh=nh):
                          box["wo"] = side_wload((8, 512), kcview(wout_s[:, nh * 512:(nh + 1) * 512]))
                      steps.append(st_w)
                      for i in range(4):
                          def st_b(i=i, nh=nh):
                              t = 4 * tb + i
                              wv, bw = box["wo"]
                              k = rot_w.next()
                              sc.pe_group([lambda kc=kc: nc.tensor.matmul(psb[k][:, :], mixT[:, tb, kc, i * 128:(i + 1) * 128], wv[:, kc, :],
                                                                          start=(kc == 0), stop=(kc == 7)) for kc in range(8)],
                                          reads=[bw] + [b_mixT[c][tb] for c in range(8)], writes=[b_ps[k]])
                              sc.op(DVE, lambda: nc.vector.scalar_tensor_tensor(
                                  out=hb[:, i, nh * 512:(nh + 1) * 512], in0=hb[:, i, nh * 512:(nh + 1) * 512], scalar=ALPHA,
                                  in1=psb[k][:, :], op0=ALU.mult, op1=ALU.add), reads=[b_ps[k]], writes=[bh[i]])
                          steps.append(st_b)
                  steps += ln_pipeline(hb, bh, GB[2], GB[3], tb, True)
                  return steps

              def E_steps(tb):
                  j = tb % 3
                  hb = hblk2[j]
                  bh = b_hblk2[j]
                  box = {}
                  steps = []
                  ln_steps = ln_pipeline(hb, bh, GB[4], GB[5], tb, True)
                  pls = []
                  pcs = []
                  for i in range(4):
                      def st_p(i=i):
                          t = 4 * tb + i
                          sc.dma(POOL, s_p[i % 2], p32[i % 2][:], p_d[b, t * 128:(t + 1) * 128, :], writes=[b_p32[i % 2]])
                      pls.append(st_p)

                      def st_pc(i=i):
                          sc.op(POOL, lambda: nc.gpsimd.tensor_copy(out=pbf[i][:], in_=p32[i % 2][:]), reads=[b_p32[i % 2]], writes=[b_pbf[i]])
                      pcs.append(st_pc)

                  def st_wpp():
                      sc.dma(POOL, s_wpp, wppt[:], kcview(wpp_s[:, :]), reads=b_prep, writes=[b_wpp])
                  steps += [pls[0], pls[1], st_wpp, ln_steps[0], pcs[0], ln_steps[1], pcs[1], pls[2], ln_steps[2], pls[3], ln_steps[3], pcs[2], ln_steps[4], pcs[3]] + ln_steps[5:]
                  for i in range(4):
                      def st_pt(i=i):
                          sc.pe_group([lambda jj=jj: nc.tensor.transpose(psT[:, jj * 128:(jj + 1) * 128],
                                                                       pbf[i][:, jj * 128:(jj + 1) * 128], ident) for jj in range(2)],
                                      reads=[b_pbf[i], b_c2], writes=[b_psT])
                          sc.op(ACT, lambda: nc.scalar.copy(out=pT[:, :, i * 128:(i + 1) * 128],
                                                            in_=psT[:, 0:256].rearrange("p (c n) -> p c n", c=2)),
                                reads=[], writes=[b_psT, b_pT[i]])
                      steps.append(st_pt)
                  for nh in range(2):
                      def st_w(nh=nh):
                          box["wpg"] = side_wload((8, 512), kcview(wpg_s[:, nh * 512:(nh + 1) * 512]))
                      steps.append(st_w)
                      for i in range(4):
                          def st_g(i=i, nh=nh):
                              t = 4 * tb + i
                              wv, bw = box["wpg"]
                              kG = rot_w.next()
                              kE = rot_w.next()
                              sc.pe_group([lambda kc=kc: nc.tensor.matmul(psb[kG][:, :], hT[:, kc, t * 128:(t + 1) * 128], wv[:, kc, :],
                                                                          start=(kc == 0), stop=(kc == 7)) for kc in range(8)],
                                          reads=[bw, b_hT[t]], writes=[b_ps[kG]])
                              sc.pe_group([lambda kc=kc: nc.tensor.matmul(psb[kE][:, :], pT[:, kc, i * 128:(i + 1) * 128],
                                                                          wppt[:, kc, nh * 512:(nh + 1) * 512], start=(kc == 0), stop=(kc == 1))
                                           for kc in range(2)], reads=[b_wpp, b_pT[i]], writes=[b_ps[kE]])
                              ei = rot_eg.next()
                              sc.op(DVE, lambda: nc.vector.tensor_tensor(out=eg[ei][:], in0=psb[kG][:, :], in1=gbp[8][:, nh * 512:(nh + 1) * 512],
                                                                         op=ALU.add), reads=[b_ps[kG], b_gbs[8]], writes=[b_eg[ei]])
                              sc.op(ACT, lambda: nc.scalar.activation(out=eg[ei][:], in_=eg[ei][:], func=AF.Sigmoid),
                                    reads=[], writes=[b_eg[ei]])
                              sc.op(DVE, lambda: nc.vector.tensor_tensor(out=eg[ei][:], in0=psb[kE][:, :], in1=eg[ei][:], op=ALU.mult),
                                    reads=[b_ps[kE]], writes=[b_eg[ei]])
                              sc.op(DVE, lambda: nc.vector.scalar_tensor_tensor(
                                  out=hb[:, i, nh * 512:(nh + 1) * 512], in0=hb[:, i, nh * 512:(nh + 1) * 512], scalar=ALPHA,
                                  in1=eg[ei][:], op0=ALU.mult, op1=ALU.add), reads=[b_eg[ei]], writes=[bh[i]])
                          steps.append(st_g)
                  def store(i):
                      t = 4 * tb + i
                      sc.dma(POOL, s_outs[j * 4 + i], out_d[b, t * 128:(t + 1) * 128, :], hb[:, i, :], reads=[bh[i]])
                  steps += ln_pipeline(hb, bh, GB[6], GB[7], tb, False, after_b=store)
                  return steps

              class Side2:
                  def __init__(self):
                      self.steps = []

                  def step(self, k=1):
                      for _ in range(k):
                          if self.steps:
                              self.steps.pop(0)()

                  def flush(self):
                      while self.steps:
                          self.steps.pop(0)()
              side2 = Side2()
              side2.steps = X_steps(0)
              side2.flush()
              for tb in range(NB):
                  cols = slice(tb * TB, (tb + 1) * TB)
                  hts = b_hT[4 * tb:4 * tb + 4]
                  hb = hblk2[tb % 3]
                  bh = b_hblk2[tb % 3]
                  es_ = E_steps(tb - 1) if tb > 0 else []
                  xs_ = X_steps(tb + 1) if tb + 1 < NB else []
                  if tb + 1 == 2:
                      for bb_ in b_hblk2[2]:
                          for c_ in range(8):
                              for q_ in range(2):
                                  mb = b_mixT[c_][q_]
                                  for tk in [mb.w] + list(mb.r.values()):
                                      if tk is not None:
                                          cur = bb_.r.get(tk[0].name)
                                          if cur is None or cur[1] < tk[1]:
                                              bb_.r[tk[0].name] = tk
                  while es_ or xs_:
                      if es_:
                          side2.steps.append(es_.pop(0))
                      if xs_:
                          side2.steps.append(xs_.pop(0))
                  fc = 0
                  for n in range(6):
                      ncol = min(512, FF - n * 512)
                      wg, bwg = wload(3, (8, ncol), kcview(wg_s[:, n * 512:n * 512 + ncol]))
                      wu, bwu = wload(3, (8, ncol), kcview(wu_s[:, n * 512:n * 512 + ncol]))
                      for f in range(ncol // 128):
                          kg_, ku_ = (0, 1) if fc % 2 == 0 else (2, 3)
                          sc.pe_group([lambda kc=kc: nc.tensor.matmul(psb[kg_][:, :], wg[:, kc, f * 128:(f + 1) * 128], hT[:, kc, cols],
                                                                      start=(kc == 0), stop=(kc == 7)) for kc in range(8)],
                                      reads=[bwg] + hts, writes=[b_ps[kg_]])
                          sc.pe_group([lambda kc=kc: nc.tensor.matmul(psb[ku_][:, :], wu[:, kc, f * 128:(f + 1) * 128], hT[:, kc, cols],
                                                                      start=(kc == 0), stop=(kc == 7)) for kc in range(8)],
                                      reads=[bwu] + hts, writes=[b_ps[ku_]])
                          ei = rot_eg.next()
                          sc.op(ACT, lambda: nc.scalar.activation(out=eg[ei][:], in_=psb[kg_][:, :], func=AF.Silu),
                                reads=[b_ps[kg_]], writes=[b_eg[ei]])
                          sc.op(DVE, lambda: nc.vector.tensor_tensor(out=gT[:, fc, :], in0=eg[ei][:], in1=psb[ku_][:, :], op=ALU.mult),
                                reads=[b_eg[ei], b_ps[ku_]], writes=[b_gT[fc]])
                          fc += 1
                          side2.step(1)
                  chk("gateup")
                  for nh in range(2):
                      for kg in range(3):
                          nk = min(8, 22 - kg * 8)
                          wd, bwd = wload(3, (nk, 512), kcview(wd_s[kg * 1024:kg * 1024 + nk * 128, nh * 512:(nh + 1) * 512]))
                          for i in range(4):
                              sc.pe_group([lambda kl=kl: nc.tensor.matmul(psb[i][:, :], gT[:, kg * 8 + kl, i * 128:(i + 1) * 128], wd[:, kl, :],
                                                                          start=(kg == 0 and kl == 0), stop=(kg == 2 and kl == nk - 1))
                                           for kl in range(nk)],
                                          reads=[bwd] + b_gT[kg * 8:kg * 8 + nk], writes=[b_ps[i]])
                              side2.step(2)
                      for i in range(4):
                          sc.op(DVE, lambda: nc.vector.scalar_tensor_tensor(
                              out=hb[:, i, nh * 512:(nh + 1) * 512], in0=hb[:, i, nh * 512:(nh + 1) * 512], scalar=ALPHA,
                              in1=psb[i][:, :], op0=ALU.mult, op1=ALU.add), reads=[b_ps[i]], writes=[bh[i]])
                  chk("down")
                  side2.flush()
              side2.steps = E_steps(NB - 1)
              side2.flush()
              sc.barrier(exclude=tuple(s_preps))
              del wsl_all[2:]
              wcount[0] = 0

    except _Stop:
        sc.barrier()
    for so in s_outs:
        if so.count > 0:
            SP.h.wait_ge(so.h, so.count)
    es.close()
    return nc


def _consts():
    c = np.zeros((128, 640), np.float32)
    idx = np.arange(128)
    c[:, 0:128] = np.eye(128, dtype=np.float32)
    c[:, 128:256] = (idx[:, None] > idx[None, :]).astype(np.float32)
    c[:, 256:384] = (idx[:, None] <= idx[None, :]).astype(np.float32)
    pm = np.zeros((128, 128), np.float32)
    for m in range(128):
        if (m % 64) < 32:
            pm[m + 32, m] = -1.0
        else:
            pm[m - 32, m] = 1.0
    c[:, 384:512] = pm
    c[:, 512:640] = (idx[:, None] < idx[None, :]).astype(np.float32)
    inv_freq = (1.0 / (np.float32(10000.0) ** (np.arange(0, 64, 2, dtype=np.float32) / np.float32(64.0)))).astype(np.float32)
    ang = np.arange(S, dtype=np.float32)[:, None] * inv_freq[None, :]
    ang = np.concatenate([ang, ang], axis=-1)
    cs = np.zeros((2, 128, S), np.float32)
    cs[0] = np.tile(np.cos(ang).T, (2, 1))
    cs[1] = np.tile(np.sin(ang).T, (2, 1))
    return c, cs


def make_in_maps(inputs, ncores=NCORES, nseq=NSEQ):
    f = lambda a: np.ascontiguousarray(np.asarray(a, dtype=np.float32))
    c, cs = _consts()
    vecs = np.stack([np.broadcast_to(f(inputs[n]).reshape(1, D), (128, D)) for n in VEC_NAMES]).astype(np.float32)
    small = np.concatenate([f(inputs["lam_q1"]).reshape(-1), f(inputs["lam_k1"]).reshape(-1), f(inputs["lam_q2"]).reshape(-1),
                            f(inputs["lam_k2"]).reshape(-1), f(inputs["subln_g"]).reshape(-1)])
    small = np.ascontiguousarray(np.broadcast_to(small[None, :], (128, 384))).astype(np.float32)
    shared = {
        "w_in": f(inputs["w_in"])[0], "w_out": f(inputs["w_out"])[0], "w_ffn_gate": f(inputs["w_ffn_gate"])[0],
        "w_ffn_up": f(inputs["w_ffn_up"])[0], "w_ffn_down": f(inputs["w_ffn_down"])[0],
        "w_ple_gate": f(inputs["w_ple_gate"])[0], "w_ple_proj": f(inputs["w_ple_proj"])[0],
        "vecs": np.ascontiguousarray(vecs), "small": small, "cbf": c, "cossin": cs,
    }
    x = f(inputs["x"])
    p = f(inputs["p"])[0]
    maps = []
    for i in range(ncores):
        m = dict(shared)
        m["x"] = np.ascontiguousarray(x[i * nseq:(i + 1) * nseq])
        m["p"] = np.ascontiguousarray(p[i * nseq:(i + 1) * nseq])
        maps.append(m)
    return maps


def kernel(**inputs):
    nc = build(NSEQ)
    in_maps = make_in_maps(inputs)
    res = run_bass_kernel_spmd(nc, in_maps, core_ids=list(range(NCORES)))
    return np.concatenate([np.asarray(r["out"], dtype=np.float32) for r in res.results], axis=0)
```
